# Optimizing a Trainium2 kernel written in Bass

```python
import math
import jax, jax.numpy as jnp
from jax import lax
import numpy as np

D_MODEL = 1024
BATCH = 8
SEQ = 4096
DEPTH = 4

CHUNK = 64
MIX_WIDTH = D_MODEL
CONV_A_WIDTH = D_MODEL // 4
CONV_A_K = 3
CONV_B_WIDTH = D_MODEL // 4
CONV_B_K = 31
SB_HEAD_DIM = 64
SB_HEADS = (MIX_WIDTH - CONV_A_WIDTH - CONV_B_WIDTH) // SB_HEAD_DIM
SB_WIDTH = SB_HEADS * SB_HEAD_DIM
SB_BLOCK = 128
D_FF = 4 * D_MODEL
IN_COLS = 3 * CONV_A_WIDTH + 2 * CONV_B_WIDTH + 3 * SB_WIDTH
RMS_EPS = 1e-6
LN_EPS = 1e-5

kernel_name = "hymba_style_conv_stickbreaking_hybrid"


def _rmsnorm(x, g):
    xf = x.astype(jnp.float32)
    y = xf * lax.rsqrt(jnp.mean(xf * xf, axis=-1, keepdims=True) + RMS_EPS)
    return (y * g.astype(jnp.float32)).astype(x.dtype)


def _layernorm(x, g, b):
    xf = x.astype(jnp.float32)
    mu = jnp.mean(xf, axis=-1, keepdims=True)
    var = jnp.mean(jnp.square(xf - mu), axis=-1, keepdims=True)
    y = (xf - mu) * lax.rsqrt(var + LN_EPS)
    return (y * g.astype(jnp.float32) + b.astype(jnp.float32)).astype(x.dtype)


def _causal_dwconv(x, w):
    k, c = w.shape
    return lax.conv_general_dilated(
        x, w[:, None, :].astype(x.dtype),
        window_strides=(1,), padding=((k - 1, 0),),
        dimension_numbers=("NWC", "WIO", "NWC"),
        feature_group_count=c)


def _stick_breaking(q, k, v):
    t_len = q.shape[2]
    scale = 1.0 / math.sqrt(q.shape[-1])
    outs = []
    for q0 in range(0, t_len, SB_BLOCK):
        kv_len = q0 + SB_BLOCK
        qb = q[:, :, q0:kv_len]
        kb = k[:, :, :kv_len]
        vb = v[:, :, :kv_len]
        z = jnp.einsum("bhqd,bhkd->bhqk", qb, kb).astype(jnp.float32) * scale
        t_idx = q0 + jnp.arange(SB_BLOCK)[:, None]
        s_idx = jnp.arange(kv_len)[None, :]
        mask = s_idx < t_idx
        log_beta = jax.nn.log_sigmoid(z)
        log_1m = jnp.where(mask, log_beta - z, 0.0)
        suffix = lax.cumsum(log_1m, axis=3, reverse=True) - log_1m
        a = jnp.where(mask, jnp.exp(log_beta + suffix), 0.0)
        outs.append(jnp.einsum("bhqk,bhkd->bhqd", a.astype(v.dtype), vb))
    return jnp.concatenate(outs, axis=2)


def _layer(x, g_attn, w_in, w_conv_a, w_conv_b, b_conv_b, ln_b_g, ln_b_b, w_out,
           g_ffn, w_ff1, w_ff2):
    bsz, t_len, _ = x.shape
    h = _rmsnorm(x, g_attn)
    p = h @ w_in
    o0 = 0
    def take(n):
        nonlocal o0
        s = p[..., o0:o0 + n]
        o0 += n
        return s
    a_b = take(CONV_A_WIDTH)
    a_c = take(CONV_A_WIDTH)
    a_h = take(CONV_A_WIDTH)
    b_glu = take(2 * CONV_B_WIDTH)
    q = take(SB_WIDTH)
    k = take(SB_WIDTH)
    v = take(SB_WIDTH)

    y_a = a_b * _causal_dwconv(a_c * a_h, w_conv_a)

    g = b_glu[..., :CONV_B_WIDTH] * jax.nn.sigmoid(b_glu[..., CONV_B_WIDTH:])
    g = _causal_dwconv(g, w_conv_b) + b_conv_b
    y_b = jax.nn.silu(_layernorm(g, ln_b_g, ln_b_b))

    def heads(t):
        return t.reshape(bsz, t_len, SB_HEADS, SB_HEAD_DIM).transpose(0, 2, 1, 3)
    o = _stick_breaking(heads(q), heads(k), heads(v))
    y_c = o.transpose(0, 2, 1, 3).reshape(bsz, t_len, SB_WIDTH)

    x = x + jnp.concatenate([y_a, y_b, y_c], axis=-1) @ w_out

    h2 = _rmsnorm(x, g_ffn)
    x = x + jnp.square(jax.nn.relu(h2 @ w_ff1)) @ w_ff2
    return x


def setup_inputs(seed: int = 0) -> dict:
    key = jax.random.key(seed)
    ks = jax.random.split(key, 14)
    f32 = jnp.float32
    def nrm(k, shape, scale):
        return jax.random.normal(k, shape, f32) * scale
    return {
        "x": nrm(ks[0], (BATCH, SEQ, D_MODEL), 1.0),
        "g_attn": 1.0 + nrm(ks[1], (DEPTH, D_MODEL), 0.02),
        "w_in": nrm(ks[2], (DEPTH, D_MODEL, IN_COLS), D_MODEL ** -0.5),
        "w_conv_a": nrm(ks[3], (DEPTH, CONV_A_K, CONV_A_WIDTH), CONV_A_K ** -0.5),
        "w_conv_b": nrm(ks[4], (DEPTH, CONV_B_K, CONV_B_WIDTH), CONV_B_K ** -0.5),
        "b_conv_b": nrm(ks[5], (DEPTH, CONV_B_WIDTH), 0.02),
        "ln_b_g": 1.0 + nrm(ks[6], (DEPTH, CONV_B_WIDTH), 0.02),
        "ln_b_b": nrm(ks[7], (DEPTH, CONV_B_WIDTH), 0.02),
        "w_out": nrm(ks[8], (DEPTH, MIX_WIDTH, D_MODEL), MIX_WIDTH ** -0.5),
        "g_ffn": 1.0 + nrm(ks[9], (DEPTH, D_MODEL), 0.02),
        "w_ff1": nrm(ks[10], (DEPTH, D_MODEL, D_FF), D_MODEL ** -0.5),
        "w_ff2": nrm(ks[11], (DEPTH, D_FF, D_MODEL), D_FF ** -0.5),
        "g_final": 1.0 + nrm(ks[12], (D_MODEL,), 0.02),
    }


def reference(x, g_attn, w_in, w_conv_a, w_conv_b, b_conv_b, ln_b_g, ln_b_b, w_out,
              g_ffn, w_ff1, w_ff2, g_final):
    for l in range(DEPTH):
        x = _layer(x, g_attn[l], w_in[l], w_conv_a[l], w_conv_b[l], b_conv_b[l],
                   ln_b_g[l], ln_b_b[l], w_out[l], g_ffn[l], w_ff1[l], w_ff2[l])
    return _rmsnorm(x, g_final)
```

```python
import contextlib
import numpy as np
import ml_dtypes
import concourse.bass as bass
import concourse.mybir as mybir
from concourse.bass_utils import run_bass_kernel_spmd

F32 = mybir.dt.float32
BF16 = mybir.dt.bfloat16
AF = mybir.ActivationFunctionType
ALU = mybir.AluOpType

D = 1024
NCOL = 2816
DFF = 4096
RMS_EPS = 1e-6
LN_EPS = 1e-5
NS = 4
IN_SLOTS = [(0, 4), (4, 8), (8, 12), (12, 16), (16, 18), (18, 22)]


class Buf:
    __slots__ = ("w", "r")

    def __init__(self):
        self.w = None
        self.r = {}


class Prog:
    ENGS = ("pe", "act", "dve", "pool", "sp")

    def __init__(self):
        self.items = {e: [] for e in self.ENGS}
        self.nops = {e: 0 for e in self.ENGS}
        self.signaled = {e: set() for e in self.ENGS}
        self.epoch = 0
        self.op_epoch = {e: [] for e in self.ENGS}
        self.dma_sems = []
        self.waited = {e: {} for e in self.ENGS}

    def op(self, eng, fn, waits=()):
        for w in waits:
            self.wait(eng, w)
        idx = self.nops[eng]
        self.nops[eng] += 1
        self.op_epoch[eng].append(self.epoch)
        self.items[eng].append(("op", fn, idx))
        return (eng, idx)

    def wait(self, eng, tok):
        if tok is None:
            return
        if tok[0] == "dma":
            _, sname, cnt = tok
            key = ("dma", sname)
            if self.waited[eng].get(key, -1) >= cnt:
                return
            self.waited[eng][key] = cnt
            self.items[eng].append(("wdma", sname, cnt))
            return
        src, idx = tok
        key = (src, self.op_epoch[src][idx])
        if self.waited[eng].get(key, -1) >= idx:
            return
        self.waited[eng][key] = idx
        self.signaled[src].add(idx)
        self.items[eng].append(("wait", src, idx))

    def new_dma_sem(self, name):
        self.dma_sems.append(name)
        return {"name": name, "cnt": 0}

    def dma(self, eng, sem, fn, waits=()):
        for w in waits:
            self.wait(eng, w)
        sem["cnt"] += 16
        self.items[eng].append(("dma", fn, sem["name"]))
        return ("dma", sem["name"], sem["cnt"])

    def use(self, eng, fn, reads=(), writes=(), dma_sem=None, nosame=False):
        waits = []
        for b in reads:
            if b.w is not None:
                waits.append(b.w)
        for b in writes:
            if b.w is not None:
                waits.append(b.w)
            for t in b.r.values():
                waits.append(t)
        fw = []
        for w in waits:
            if w[0] != "dma" and w[0] == eng and (eng == "pe" or nosame):
                continue
            fw.append(w)
        if dma_sem is not None:
            tok = self.dma(eng, dma_sem, fn, waits=fw)
        else:
            tok = self.op(eng, fn, waits=fw)
        key = ("dma", tok[1]) if tok[0] == "dma" else tok[0]
        for b in reads:
            b.r[key] = tok
        for b in writes:
            b.w = tok
            b.r = {}
        return tok

    def emit(self, nc, n_epochs):
        counts = {}
        for e in self.ENGS:
            c = {}
            for idx in range(self.nops[e]):
                if idx in self.signaled[e]:
                    ep = self.op_epoch[e][idx]
                    c[ep] = c.get(ep, 0) + 1
                    counts[(e, idx)] = c[ep]
        with contextlib.ExitStack() as st:
            sems = {}
            for e in self.ENGS:
                for ep in range(n_epochs):
                    sems[(e, ep)] = st.enter_context(nc.semaphore(f"s_{e}_{ep}"))
            dsems = {n: st.enter_context(nc.semaphore(f"d_{n}")) for n in self.dma_sems}
            block = st.enter_context(nc.Block())

            def run(engname):
                def body(eng):
                    for it in self.items[engname]:
                        k = it[0]
                        if k == "op":
                            ins = it[1](eng)
                            idx = it[2]
                            if idx in self.signaled[engname]:
                                ins.then_inc(sems[(engname, self.op_epoch[engname][idx])], 1)
                        elif k == "wait":
                            _, src, idx = it
                            eng.wait_ge(sems[(src, self.op_epoch[src][idx])], counts[(src, idx)])
                        elif k == "wdma":
                            eng.wait_ge(dsems[it[1]], it[2])
                        else:
                            it[1](eng).then_inc(dsems[it[2]], 16)
                return body

            block.tensor(run("pe"))
            block.scalar(run("act"))
            block.vector(run("dve"))
            block.gpsimd(run("pool"))
            block.sync(run("sp"))


def build(T, DEPTH):
    NT = T // 512
    NB = T // 128
    nc = bass.Bass("TRN2", target_bir_lowering=False)
    dt = lambda n, s, d, k: nc.dram_tensor(n, s, d, kind=k).ap()
    x_in = dt("x", [T, D], F32, "ExternalInput")
    g_attn = dt("g_attn", [DEPTH, D], F32, "ExternalInput")
    w_in = dt("w_in", [DEPTH, D, NCOL], F32, "ExternalInput")
    w_conv_a = dt("w_conv_a", [DEPTH, 3, 256], F32, "ExternalInput")
    w_conv_b = dt("w_conv_b", [DEPTH, 31, 256], F32, "ExternalInput")
    b_conv_b = dt("b_conv_b", [DEPTH, 256], F32, "ExternalInput")
    ln_b_g = dt("ln_b_g", [DEPTH, 256], F32, "ExternalInput")
    ln_b_b = dt("ln_b_b", [DEPTH, 256], F32, "ExternalInput")
    w_out = dt("w_out", [DEPTH, D, D], F32, "ExternalInput")
    g_ffn = dt("g_ffn", [DEPTH, D], F32, "ExternalInput")
    w_ff1 = dt("w_ff1", [DEPTH, D, DFF], F32, "ExternalInput")
    w_ff2 = dt("w_ff2", [DEPTH, DFF, D], F32, "ExternalInput")
    g_final = dt("g_final", [1, D], F32, "ExternalInput")
    c_ident_bf = dt("c_ident_bf", [128, 128], BF16, "ExternalInput")
    c_ident_f = dt("c_ident_f", [128, 128], F32, "ExternalInput")
    c_negtri = dt("c_negtri", [128, 128], BF16, "ExternalInput")
    c_negones = dt("c_negones", [128, 128], BF16, "ExternalInput")
    c_cmask = dt("c_cmask", [128, 128], BF16, "ExternalInput")
    c_avg = dt("c_avg", [128, 128], F32, "ExternalInput")
    out = dt("out", [T, D], F32, "ExternalOutput")
    xres = dt("xres", [T, D], F32, "Internal")

    P = Prog()
    with contextlib.ExitStack() as st:
        sb = lambda n, s, d: st.enter_context(nc.sbuf_tensor(n, s, d))
        pst = lambda n, s, d: st.enter_context(nc.psum_tensor(n, s, d))

        ring = [sb(f"ring{s}", [128, 4096], BF16) for s in range(NS)]
        ringB = [Buf() for _ in range(NS)]
        kT = sb("kT", [128, 4, T], BF16)
        kTB = [[Buf() for _ in range(NT)] for _ in range(4)]
        Vc = sb("Vc", [128, NB, 512], BF16)
        VcB = [Buf() for _ in range(NB)]
        xt = sb("xt", [128, 4, D], F32)
        xtB = [Buf() for _ in range(4)]
        hn2 = [sb(f"hn{k}", [128, D], BF16) for k in range(2)]
        hn2B = [Buf(), Buf()]
        hT = sb("hT", [128, 8, 512], BF16)
        hTB = Buf()
        qm = [sb(f"qm{k}", [128, 4, 2, 512], BF16) for k in range(2)]
        qmB = [[Buf() for _ in range(4)] for _ in range(2)]
        yT = [sb(f"yT{k}", [128, 8, 512], BF16) for k in range(2)]
        yTB = [[Buf() for _ in range(8)] for _ in range(2)]
        uT = [sb(f"uT{k}", [128, 4, 512], BF16) for k in range(2)]
        uTB = [Buf(), Buf()]
        NSCR = 5
        scr = [sb(f"scr{k}", [128, 512], F32) for k in range(NSCR)]
        scrB = [Buf() for _ in range(NSCR)]
        mbuf = sb("mbuf", [128, 2, 514], F32)
        mbufB = [Buf(), Buf()]
        ubuf = sb("ubuf", [128, 2, 542], F32)
        ubufB = [Buf(), Buf()]
        gcv = [sb(f"gcv{k}", [128, 512], F32) for k in range(2)]
        gcvB = [Buf(), Buf()]
        Eb = [sb(f"Eb{k}", [128, 512], F32) for k in range(2)]
        EbB = [Buf(), Buf()]
        Lb = [sb(f"Lb{k}", [128, 512], BF16) for k in range(2)]
        LbB = [Buf(), Buf()]
        Ab = [sb(f"Ab{k}", [128, 512], BF16) for k in range(2)]
        AbB = [Buf(), Buf()]
        Ls = [sb(f"Ls{k}", [128, 512], BF16) for k in range(2)]
        LsB = [Buf(), Buf()]
        ident_bf = sb("ident_bf", [128, 128], BF16)
        ident_f = sb("ident_f", [128, 128], F32)
        negtri = sb("negtri", [128, 128], BF16)
        negones = sb("negones", [128, 128], BF16)
        cmask = sb("cmask", [128, 128], BF16)
        avgm = sb("avgm", [128, 128], F32)
        constB = Buf()
        gFin = sb("gFin", [128, D], F32)
        gFinB = Buf()
        prm = sb("prm", [37, 256], F32)
        prmB = Buf()
        gst = sb("gst", [8, 2, 128], F32)
        gstB = Buf()
        prmT = sb("prmT", [128, 2, 2, 37], F32)
        prmTB2 = [Buf(), Buf()]
        gT = sb("gT", [128, 2, 2, 8], F32)
        gTB2 = [Buf(), Buf()]
        stat = sb("stat", [128, 16], F32)
        statB = Buf()

        bank = [pst(f"bank{k}", [128, 512], F32) for k in range(5)]
        pT = pst("pT", [128, D], BF16)
        bank += [None, pst("bank6", [128, 512], F32), pst("bank7", [128, 512], F32)]
        bankB = [Buf() for _ in range(8)]
        pTB = bankB[5]
        PSA = (0, 1)
        PSB = (2, 3)
        PSO = 4
        DN = (6, 7)

        s_ring = [P.new_dma_sem(f"ring{s}") for s in range(NS)]
        s_x = [P.new_dma_sem(f"x{b}") for b in range(4)]
        s_st = [P.new_dma_sem(f"st{b}") for b in range(4)]
        s_c = P.new_dma_sem("c")
        s_gf = P.new_dma_sem("gf")
        s_gs = P.new_dma_sem("gs")
        s_p = P.new_dma_sem("p")

        xresB = [[Buf() for _ in range(4)] for _ in range(NT)]

        NTOT = DEPTH * NT
        dorder = []
        for n in range(NTOT):
            l_, i_ = divmod(n, NT)
            if i_ == 0:
                dorder.append(("A", l_))
            if n > 0:
                dorder.append(("C", (n - 1) // NT))
            if (n + 1) % NT != 0:
                dorder.append(("A", l_))
        dorder.append(("C", DEPTH - 1))
        wseq = []
        for ph, l in dorder:
            if ph == "A":
                for g in range(6):
                    wseq.append(("in", l, g))
            else:
                for eh in range(2):
                    wseq.append(("out", l, eh))
                order = [("w1", 0), ("w1", 1)]
                for g in range(2, 8):
                    order += [("w2", g - 2), ("w1", g)]
                order += [("w2", 6), ("w2", 7)]
                for kind, g in order:
                    wseq.append((kind, l, g))
        wstate = {"loaded": 0}

        def author_load(k):
            kind, l, g = wseq[k]
            s = k % NS
            slot = ring[s]
            if kind == "in":
                lo, hi = IN_SLOTS[g]
                n = (hi - lo) * 128
                dst = slot[:].rearrange("p (c n) -> p c n", c=8)[:, :, 0:n]
                src = w_in[l].rearrange("(c p) n -> p c n", p=128)[:, :, lo * 128:hi * 128]
            elif kind == "out":
                dst = slot[:].rearrange("p (c n) -> p c n", c=8)
                src = w_out[l].rearrange("(c p) n -> p c n", p=128)[:, :, g * 512:(g + 1) * 512]
            elif kind == "w1":
                dst = slot[:].rearrange("p (c n) -> p c n", c=8)
                src = w_ff1[l].rearrange("(c p) n -> p c n", p=128)[:, :, g * 512:(g + 1) * 512]
            else:
                dst = slot[:].rearrange("p (c n) -> p c n", c=4)
                src = w_ff2[l][g * 512:(g + 1) * 512, :].rearrange("(c p) n -> p c n", p=128)
            P.use("pool", lambda e, dst=dst, src=src: e.dma_start(out=dst, in_=src),
                  writes=[ringB[s]], dma_sem=s_ring[s])

        def get_w(k, kind=None):
            while wstate["loaded"] < min(len(wseq), NS):
                author_load(wstate["loaded"])
                wstate["loaded"] += 1
            assert wstate["loaded"] > k, (k, wstate)
            if kind is not None:
                assert wseq[k][0] == kind[0] and wseq[k][2] == kind[1], (wseq[k], kind)
            return k % NS

        def release(k):
            nk = k + NS
            if nk < len(wseq):
                assert wstate["loaded"] == nk, (k, wstate)
                author_load(nk)
                wstate["loaded"] += 1

        for dst_t, src in ((ident_bf, c_ident_bf), (ident_f, c_ident_f), (negtri, c_negtri),
                           (negones, c_negones), (cmask, c_cmask), (avgm, c_avg)):
            P.use("sp", lambda e, d=dst_t, s_=src: e.dma_start(out=d[:], in_=s_), writes=[constB], dma_sem=s_c)
        P.use("sp", lambda e: e.dma_start(out=gFin[:], in_=g_final.partition_broadcast(128)), writes=[gFinB], dma_sem=s_gf)
        for k in range(2):
            for j in range(4):
                P.use("dve", lambda e, j=j, k=k: e.memset(qm[k][:, j, :, :], 0.0), writes=[qmB[k][j]])

        dctr = {"d": 0, "ev": 0}
        pend = []

        def flush():
            while pend:
                pend.pop(0)()

        def next_bank():
            b = DN[dctr["d"] % 2]
            dctr["d"] += 1
            return b

        def evac_eng():
            dctr["ev"] += 1
            return "act" if dctr["ev"] % 2 else "dve"

        def copy_op(eng, out_ap, in_ap, reads, writes):
            if eng == "act":
                return P.use("act", lambda e: e.activation(out=out_ap, in_=in_ap, func=AF.Copy), reads=reads, writes=writes)
            return P.use("dve", lambda e: e.tensor_copy(out=out_ap, in_=in_ap), reads=reads, writes=writes)

        def rms_stats(b):
            c0 = 3 * b
            P.use("dve", lambda e: e.memset(stat[:, c0:c0 + 1], 0.0), writes=[statB])
            P.use("act", lambda e: e.activation(out=hn2[b % 2][:], in_=xt[:, b, :], func=AF.Square, accum_out=stat[:, c0:c0 + 1]),
                  reads=[xtB[b]], writes=[statB, hn2B[b % 2]])
            P.use("act", lambda e: e.activation(out=stat[:, c0 + 1:c0 + 2], in_=stat[:, c0:c0 + 1], func=AF.Ln,
                                                scale=1.0 / D, bias=RMS_EPS), reads=[statB], writes=[statB])
            P.use("act", lambda e: e.activation(out=stat[:, c0 + 2:c0 + 3], in_=stat[:, c0 + 1:c0 + 2], func=AF.Exp,
                                                scale=-0.5), reads=[statB], writes=[statB])
            return stat[:, c0 + 2:c0 + 3]

        def norm_gen(lp, gk):
            gTB = gTB2[lp]
            rs = []
            for b in range(4):
                rs.append(rms_stats(b))
                yield 4
            gb = gT[:, lp, gk, :].unsqueeze(2).to_broadcast([128, 8, 128])
            for b in range(4):
                hn, hnB = hn2[b % 2], hn2B[b % 2]
                P.use("dve", lambda e, b=b, r=rs[b], hn=hn: e.tensor_scalar_mul(out=hn[:], in0=xt[:, b, :], scalar1=r),
                      reads=[xtB[b], statB], writes=[hnB])
                for c in range(8):
                    P.use("pe", lambda e, c=c, hn=hn: e.transpose(out=pT[:, c * 128:(c + 1) * 128],
                                                                  in_=hn[:, c * 128:(c + 1) * 128], identity=ident_bf[:]),
                          reads=[hnB, constB], writes=[pTB])
                yield 6
                P.use("dve", lambda e, b=b: e.tensor_tensor(out=hT[:, :, b * 128:(b + 1) * 128],
                                                            in0=pT[:].rearrange("p (c n) -> p c n", c=8), in1=gb, op=ALU.mult),
                      reads=[pTB, gTB], writes=[hTB])
                yield 6

        wk = [0]

        def load_x(l, i):
            src = x_in if l == 0 else xres
            for b in range(4):
                rd = [] if l == 0 else [xresB[i][b]]
                r0 = i * 512 + b * 128
                P.use("sp", lambda e, b=b, r0=r0: e.dma_start(out=xt[:, b, :], in_=src[r0:r0 + 128, :]),
                      reads=rd, writes=[xtB[b]], dma_sem=s_x[b])

        def load_params(l):
            lp = l % 2
            prmTB, gTB = prmTB2[lp], gTB2[lp]
            for (r0, r1, src) in ((0, 3, w_conv_a[l]), (3, 34, w_conv_b[l]), (34, 35, b_conv_b[l:l + 1, :]),
                                  (35, 36, ln_b_g[l:l + 1, :]), (36, 37, ln_b_b[l:l + 1, :])):
                P.use("sp", lambda e, r0=r0, r1=r1, src=src: e.dma_start(out=prm[r0:r1, :], in_=src),
                      writes=[prmB], dma_sem=s_p)
            for k, gsrc in enumerate((g_attn, g_ffn)):
                P.use("sp", lambda e, k=k, gsrc=gsrc: e.dma_start(
                    out=gst[:, k, :], in_=gsrc[l:l + 1, :].rearrange("o (c p) -> (o c) p", p=128)),
                    writes=[gstB], dma_sem=s_gs)
            for c in range(2):
                bk = next_bank()
                P.use("pe", lambda e, c=c, bk=bk: e.transpose(out=bank[bk][:, 0:37], in_=prm[0:37, c * 128:(c + 1) * 128],
                                                              identity=ident_f[0:37, 0:37]),
                      reads=[prmB, constB], writes=[bankB[bk]])
                P.use("dve", lambda e, c=c, bk=bk: e.tensor_copy(out=prmT[:, lp, c, :], in_=bank[bk][:, 0:37]),
                      reads=[bankB[bk]], writes=[prmTB])
            for k in range(2):
                bk = next_bank()
                P.use("pe", lambda e, k=k, bk=bk: e.transpose(out=bank[bk][:, 0:8], in_=gst[0:8, k, :],
                                                              identity=ident_f[0:8, 0:8]),
                      reads=[gstB, constB], writes=[bankB[bk]])
                P.use("dve", lambda e, k=k, bk=bk: e.tensor_copy(out=gT[:, lp, k, :], in_=bank[bk][:, 0:8]),
                      reads=[bankB[bk]], writes=[gTB])
            for c in range(2):
                P.use("dve", lambda e, c=c: e.memset(mbuf[:, c, 0:2], 0.0), writes=[mbufB[c]])
                P.use("dve", lambda e, c=c: e.memset(ubuf[:, c, 0:30], 0.0), writes=[ubufB[c]])


        def gen_Ah(l, i):
            par = i % 2
            lp = l % 2
            prmTB = prmTB2[lp]
            pv = lambda c, r: prmT[:, lp, c, r:r + 1]
            load_x(l, i)
            for _ in range(4):
                yield 8
            for w in norm_gen(l % 2, 0):
                yield w
            wbase = wk[0]
            wk[0] += 6

            def mm_chunk(cc):
                g = [k for k, (lo, hi) in enumerate(IN_SLOTS) if lo <= cc < hi][0]
                off = cc - IN_SLOTS[g][0]
                s = get_w(wbase + g, ("in", g))
                wv = ring[s][:].rearrange("p (c n) -> p c n", c=8)
                bk = next_bank()
                for c in range(8):
                    P.use("pe", lambda e, c=c: e.matmul(
                        bank[bk][:], lhsT=wv[:, c, off * 128:(off + 1) * 128], rhs=hT[:, c, :],
                        start=(c == 0), stop=(c == 7)),
                        reads=[ringB[s], hTB], writes=[bankB[bk]])
                return bk

            for c in range(2):
                bk = mm_chunk(2 + c)
                flush()
                pend.append(lambda bk=bk: P.use("act", lambda e: e.activation(out=scr[0][:], in_=bank[bk][:], func=AF.Copy),
                                                reads=[bankB[bk]], writes=[scrB[0]]))
                yield 8
                bk = mm_chunk(4 + c)
                flush()

                def ev_ah(bk=bk, c=c):
                    P.use("dve", lambda e: e.tensor_tensor(out=mbuf[:, c, 2:514], in0=bank[bk][:], in1=scr[0][:], op=ALU.mult),
                          reads=[bankB[bk], scrB[0]], writes=[mbufB[c]])
                    P.use("dve", lambda e: e.tensor_scalar_mul(out=scr[1][:], in0=mbuf[:, c, 0:512], scalar1=pv(c, 0)),
                          reads=[mbufB[c], prmTB], writes=[scrB[1]])
                    P.use("dve", lambda e: e.scalar_tensor_tensor(out=scr[2][:], in0=mbuf[:, c, 1:513], scalar=pv(c, 1), in1=scr[1][:],
                                                                  op0=ALU.mult, op1=ALU.add),
                          reads=[mbufB[c], scrB[1]], writes=[scrB[2]])
                    P.use("dve", lambda e: e.scalar_tensor_tensor(out=scr[1][:], in0=mbuf[:, c, 2:514], scalar=pv(c, 2), in1=scr[2][:],
                                                                  op0=ALU.mult, op1=ALU.add),
                          reads=[mbufB[c], scrB[2]], writes=[scrB[1]])
                pend.append(ev_ah)
                yield 8
                bk = mm_chunk(c)
                flush()

                def ev_ab(bk=bk, c=c):
                    P.use("dve", lambda e: e.tensor_tensor(out=yT[par][:, c, :], in0=bank[bk][:], in1=scr[1][:], op=ALU.mult),
                          reads=[bankB[bk], scrB[1]], writes=[yTB[par][c]])
                    P.use("dve", lambda e: e.tensor_copy(out=mbuf[:, c, 0:2], in_=mbuf[:, c, 512:514]),
                          reads=[mbufB[c]], writes=[mbufB[c]])
                pend.append(ev_ab)
                yield 8
            release(wbase + 0)
            for c in range(2):
                bk = mm_chunk(6 + c)
                flush()
                pend.append(lambda bk=bk: P.use("act", lambda e: e.activation(out=scr[0][:], in_=bank[bk][:], func=AF.Copy),
                                                reads=[bankB[bk]], writes=[scrB[0]]))
                yield 8
                bk = mm_chunk(8 + c)
                flush()

                def ev_gate(bk=bk, c=c):
                    P.use("act", lambda e: e.activation(out=scr[1][:], in_=bank[bk][:], func=AF.Exp, scale=-1.0),
                          reads=[bankB[bk]], writes=[scrB[1]])
                    P.use("dve", lambda e: e.tensor_scalar_add(out=scr[2][:], in0=scr[1][:], scalar1=1.0),
                          reads=[scrB[1]], writes=[scrB[2]])
                    P.use("dve", lambda e: e.reciprocal(out=scr[1][:], in_=scr[2][:]), reads=[scrB[2]], writes=[scrB[1]])
                    P.use("dve", lambda e: e.tensor_tensor(out=ubuf[:, c, 30:542], in0=scr[0][:], in1=scr[1][:], op=ALU.mult),
                          reads=[scrB[0], scrB[1]], writes=[ubufB[c]])
                pend.append(ev_gate)
                yield 8
            release(wbase + 1)
            flush()
            for j in range(4):
                bk = mm_chunk(10 + j)
                flush()

                def ev_q(bk=bk, j=j):
                    for hh in range(2):
                        P.use("act", lambda e, hh=hh: e.mul(out=qm[par][hh * 64:(hh + 1) * 64, j, hh, :],
                                                            in_=bank[bk][hh * 64:(hh + 1) * 64, :], mul=0.125),
                              reads=[bankB[bk]], writes=[qmB[par][j]])
                pend.append(ev_q)
                if j == 1:
                    release(wbase + 2)
                yield 8
            for j in range(4):
                bk = mm_chunk(14 + j)
                flush()
                pend.append(lambda bk=bk, j=j: copy_op("dve", kT[:, j, i * 512:(i + 1) * 512], bank[bk][:],
                                                       reads=[bankB[bk]], writes=[kTB[j][i]]))
                if j == 1:
                    release(wbase + 3)
                if j == 3:
                    release(wbase + 4)
                yield 8
            s5 = get_w(wbase + 5, ("in", 5))
            w5 = ring[s5][:].rearrange("p (c n) -> p c n", c=8)
            for b in range(4):
                bk = next_bank()
                for c in range(8):
                    P.use("pe", lambda e, c=c, bk=bk, b=b: e.matmul(
                        bank[bk][:], lhsT=hT[:, c, b * 128:(b + 1) * 128], rhs=w5[:, c, :],
                        start=(c == 0), stop=(c == 7)), reads=[ringB[s5], hTB], writes=[bankB[bk]])
                flush()
                pend.append(lambda bk=bk, b=b: copy_op("dve", Vc[:, 4 * i + b, :], bank[bk][:],
                                                       reads=[bankB[bk]], writes=[VcB[4 * i + b]]))
                yield 8
            release(wbase + 5)
            flush()
            yield 1

        def gen_At(l, i):
            par = i % 2
            lp = l % 2
            prmTB = prmTB2[lp]
            pv = lambda c, r: prmT[:, lp, c, r:r + 1]
            chain = []
            for c in range(2):
                chain.append(lambda c=c: P.use("dve", lambda e: e.tensor_scalar(
                    out=gcv[c][:], in0=ubuf[:, c, 0:512], scalar1=pv(c, 3), scalar2=pv(c, 34), op0=ALU.mult, op1=ALU.add),
                    reads=[ubufB[c], prmTB], writes=[gcvB[c]]))
            for k in range(1, 31):
                for c in range(2):
                    chain.append(lambda c=c, k=k: P.use("dve", lambda e: e.scalar_tensor_tensor(
                        out=gcv[c][:], in0=ubuf[:, c, k:k + 512], scalar=pv(c, 3 + k), in1=gcv[c][:], op0=ALU.mult, op1=ALU.add),
                        reads=[ubufB[c], gcvB[c]], writes=[gcvB[c]]))
            for c in range(2):
                chain.append(lambda c=c: P.use("dve", lambda e: e.tensor_copy(out=ubuf[:, c, 0:30], in_=ubuf[:, c, 512:542]),
                                               reads=[ubufB[c]], writes=[ubufB[c]]))

            def do_chain(n):
                for _ in range(n):
                    if chain:
                        chain.pop(0)()

            while chain:
                do_chain(4)
                yield 13
            flush()
            bm = next_bank()
            for c in range(2):
                P.use("pe", lambda e, c=c: e.matmul(bank[bm][:], lhsT=avgm[:], rhs=gcv[c][:], start=(c == 0), stop=(c == 1)),
                      reads=[gcvB[c], constB], writes=[bankB[bm]])
            for c in range(2):
                P.use("dve", lambda e, c=c: e.tensor_tensor(out=gcv[c][:], in0=gcv[c][:], in1=bank[bm][:], op=ALU.subtract),
                      reads=[gcvB[c], bankB[bm]], writes=[gcvB[c]])
                P.use("dve", lambda e, c=c: e.tensor_tensor(out=scr[3 + c][:], in0=gcv[c][:], in1=gcv[c][:], op=ALU.mult),
                      reads=[gcvB[c]], writes=[scrB[3 + c]])
            yield 8
            yield 4
            flush()
            bv = next_bank()
            for c in range(2):
                P.use("pe", lambda e, c=c: e.matmul(bank[bv][:], lhsT=avgm[:], rhs=scr[3 + c][:], start=(c == 0), stop=(c == 1)),
                      reads=[scrB[3 + c], constB], writes=[bankB[bv]])
            P.use("act", lambda e: e.activation(out=scr[1][:], in_=bank[bv][:], func=AF.Ln, bias=LN_EPS),
                  reads=[bankB[bv]], writes=[scrB[1]])
            P.use("act", lambda e: e.activation(out=scr[0][:], in_=scr[1][:], func=AF.Exp, scale=-0.5),
                  reads=[scrB[1]], writes=[scrB[0]])
            yield 8
            for c in range(2):
                P.use("dve", lambda e, c=c: e.tensor_tensor(out=gcv[c][:], in0=gcv[c][:], in1=scr[0][:], op=ALU.mult),
                      reads=[gcvB[c], scrB[0]], writes=[gcvB[c]])
                P.use("dve", lambda e, c=c: e.tensor_scalar(out=gcv[c][:], in0=gcv[c][:], scalar1=pv(c, 35), scalar2=pv(c, 36),
                                                             op0=ALU.mult, op1=ALU.add),
                      reads=[gcvB[c], prmTB], writes=[gcvB[c]])
                P.use("act", lambda e, c=c: e.activation(out=scr[1 + c][:], in_=gcv[c][:], func=AF.Exp, scale=-1.0),
                      reads=[gcvB[c]], writes=[scrB[1 + c]])
                P.use("dve", lambda e, c=c: e.tensor_scalar_add(out=scr[3 + c][:], in0=scr[1 + c][:], scalar1=1.0),
                      reads=[scrB[1 + c]], writes=[scrB[3 + c]])
                P.use("dve", lambda e, c=c: e.reciprocal(out=scr[1 + c][:], in_=scr[3 + c][:]), reads=[scrB[3 + c]], writes=[scrB[1 + c]])
                P.use("dve", lambda e, c=c: e.tensor_tensor(out=yT[par][:, 2 + c, :], in0=gcv[c][:], in1=scr[1 + c][:], op=ALU.mult),
                      reads=[gcvB[c], scrB[1 + c]], writes=[yTB[par][2 + c]])
                yield 2

        WAH = 6 + 56 + 48 + 32 + 96 + 1
        WAT = 17 * 13 + 8 + 4 + 8 + 4

        def gen_C(l, i):
            par = i % 2
            load_x(l, i)
            yield 3

            def add_x(bk, b, eh):
                P.use("dve", lambda e: e.tensor_tensor(
                    out=xt[:, b, eh * 512:(eh + 1) * 512], in0=bank[bk][:], in1=xt[:, b, eh * 512:(eh + 1) * 512], op=ALU.add),
                    reads=[bankB[bk], xtB[b]], writes=[xtB[b]])

            for eh in range(2):
                s = get_w(wk[0], ("out", eh))
                wv = ring[s][:].rearrange("p (c n) -> p c n", c=8)
                for b in range(4):
                    bk = next_bank()
                    for c in range(8):
                        P.use("pe", lambda e, c=c, bk=bk, b=b, wv=wv: e.matmul(
                            bank[bk][:], lhsT=yT[par][:, c, b * 128:(b + 1) * 128], rhs=wv[:, c, :],
                            start=(c == 0), stop=(c == 7)), reads=[ringB[s], yTB[par][c]], writes=[bankB[bk]])
                    flush()
                    pend.append(lambda bk=bk, b=b, eh=eh: add_x(bk, b, eh))
                    yield 8
                release(wk[0])
                wk[0] += 1
            flush()
            for w in norm_gen(l % 2, 1):
                yield w

            def ffn1(g):
                s = get_w(wk[0], ("w1", g))
                wv = ring[s][:].rearrange("p (c n) -> p c n", c=8)
                ub = g % 2
                for f4 in range(4):
                    bk = next_bank()
                    for c in range(8):
                        P.use("pe", lambda e, c=c, bk=bk, f4=f4: e.matmul(
                            bank[bk][:], lhsT=wv[:, c, f4 * 128:(f4 + 1) * 128], rhs=hT[:, c, :],
                            start=(c == 0), stop=(c == 7)), reads=[ringB[s], hTB], writes=[bankB[bk]])
                    flush()

                    def ev(bk=bk, f4=f4):
                        P.use("dve", lambda e: e.tensor_scalar_max(out=uT[ub][:, f4, :], in0=bank[bk][:], scalar1=0.0),
                              reads=[bankB[bk]], writes=[uTB[ub]])
                        P.use("dve", lambda e: e.tensor_tensor(out=uT[ub][:, f4, :], in0=bank[bk][:], in1=uT[ub][:, f4, :], op=ALU.mult),
                              reads=[bankB[bk], uTB[ub]], writes=[uTB[ub]])
                    pend.append(ev)
                    yield 8
                release(wk[0])
                wk[0] += 1

            def ffn2(g):
                s = get_w(wk[0], ("w2", g))
                wv2 = ring[s][:].rearrange("p (c n) -> p c n", c=4)
                ub = g % 2
                flush()
                for eh in range(2):
                    for b in range(4):
                        bk = next_bank()
                        for f4 in range(4):
                            P.use("pe", lambda e, f4=f4, bk=bk, b=b, eh=eh: e.matmul(
                                bank[bk][:], lhsT=uT[ub][:, f4, b * 128:(b + 1) * 128], rhs=wv2[:, f4, eh * 512:(eh + 1) * 512],
                                start=(f4 == 0), stop=(f4 == 3)), reads=[ringB[s], uTB[ub]], writes=[bankB[bk]])
                        flush()
                        pend.append(lambda bk=bk, b=b, eh=eh: add_x(bk, b, eh))
                        yield 4
                release(wk[0])
                wk[0] += 1

            yield from ffn1(0)
            yield from ffn1(1)
            for g in range(2, 8):
                yield from ffn2(g - 2)
                yield from ffn1(g)
            yield from ffn2(6)
            yield from ffn2(7)
            flush()
            if l == DEPTH - 1:
                rs = [rms_stats(b) for b in range(4)]
                for b in range(4):
                    P.use("dve", lambda e, b=b, r=rs[b]: e.scalar_tensor_tensor(
                        out=xt[:, b, :], in0=xt[:, b, :], scalar=r, in1=gFin[:], op0=ALU.mult, op1=ALU.mult),
                        reads=[xtB[b], statB, gFinB], writes=[xtB[b]])
                dst = out
            else:
                dst = xres
            for b in range(4):
                wr = [] if l == DEPTH - 1 else [xresB[i][b]]
                r0 = i * 512 + b * 128
                P.use("sp", lambda e, b=b, r0=r0: e.dma_start(out=dst[r0:r0 + 128, :], in_=xt[:, b, :]),
                      reads=[xtB[b]], writes=wr, dma_sem=s_st[b])
            yield 2

        WC_TOTAL = 3 + 64 + 56 + 256 + 256 + 2

        def att_steps(l, i):
            par = i % 2
            units = []
            for j in range(4):
                for hh in range(2):
                    for kb in range(4 * i + 3, -1, -1):
                        units.append((j, hh, kb))
            n_u = len(units)

            def geom(u):
                j, hh, kb = units[u]
                r = kb - 4 * i
                c0 = r * 128 if r >= 0 else 0
                return j, hh, kb, r, c0, (kb == 4 * i + 3), (kb == 0)

            PS = (0, 1, 2, 3)

            def QK(u):
                j, hh, kb, r, c0, first, last = geom(u)
                bk = PS[u % 4]
                P.use("pe", lambda e: e.matmul(bank[bk][:, c0:512], lhsT=kT[:, j, kb * 128:(kb + 1) * 128],
                                               rhs=qm[par][:, j, hh, c0:512], start=True, stop=False, skip_group_check=True),
                      reads=[kTB[j][kb // 4], qmB[par][j]], writes=[bankB[bk]])

            def E(u):
                j, hh, kb, r, c0, first, last = geom(u)
                bk = PS[u % 4]
                P.use("act", lambda e: e.activation(out=Eb[u % 2][:, c0:512], in_=bank[bk][:, c0:512], func=AF.Exp),
                      reads=[bankB[bk]], writes=[EbB[u % 2]])

            def L(u):
                j, hh, kb, r, c0, first, last = geom(u)
                P.use("act", lambda e: e.activation(out=Lb[u % 2][:, c0:512], in_=Eb[u % 2][:, c0:512], func=AF.Ln, bias=1.0),
                      reads=[EbB[u % 2]], writes=[LbB[u % 2]])
                if r >= 0:
                    P.use("dve", lambda e: e.tensor_tensor(out=Lb[u % 2][:, c0:c0 + 128], in0=Lb[u % 2][:, c0:c0 + 128],
                                                           in1=cmask[:], op=ALU.mult),
                          reads=[LbB[u % 2], constB], writes=[LbB[u % 2]])

            def S2(u):
                j, hh, kb, r, c0, first, last = geom(u)
                bk = PS[u % 4]
                lsi = (j * 2 + hh) % 2
                if first:
                    P.use("dve", lambda e: e.memset(Ls[lsi][:], 0.0), writes=[LsB[lsi]])
                P.use("pe", lambda e: e.matmul(bank[bk][:, c0:512], lhsT=negtri[:], rhs=Lb[u % 2][:, c0:512],
                                               start=False, stop=first, skip_group_check=True),
                      reads=[LbB[u % 2], constB, bankB[bk]], writes=[bankB[bk]])
                if not first:
                    P.use("pe", lambda e: e.matmul(bank[bk][:, c0:512], lhsT=negones[:], rhs=Ls[lsi][:, c0:512],
                                                   start=False, stop=True, skip_group_check=True),
                          reads=[LsB[lsi], constB], writes=[bankB[bk]])
                if not last:
                    P.use("dve", lambda e: e.tensor_tensor(out=Ls[lsi][:, c0:512], in0=Ls[lsi][:, c0:512],
                                                           in1=Lb[u % 2][:, c0:512], op=ALU.add),
                          reads=[LsB[lsi], LbB[u % 2]], writes=[LsB[lsi]])

            def A(u):
                j, hh, kb, r, c0, first, last = geom(u)
                bk = PS[u % 4]
                P.use("act", lambda e: e.activation(out=Ab[u % 2][:, c0:512], in_=bank[bk][:, c0:512], func=AF.Exp),
                      reads=[bankB[bk]], writes=[AbB[u % 2]])
                if r >= 0:
                    P.use("dve", lambda e: e.tensor_tensor(out=Ab[u % 2][:, c0:c0 + 128], in0=Ab[u % 2][:, c0:c0 + 128],
                                                           in1=cmask[:], op=ALU.mult),
                          reads=[AbB[u % 2], constB], writes=[AbB[u % 2]])

            def S3(u):
                j, hh, kb, r, c0, first, last = geom(u)
                bk = PSO
                vv = Vc[:, kb, j * 128:(j + 1) * 128]
                P.use("pe", lambda e: e.matmul(bank[bk][:, c0:512], lhsT=vv, rhs=Ab[u % 2][:, c0:512],
                                               start=first, stop=last, skip_group_check=True),
                      reads=[VcB[kb], AbB[u % 2]], writes=[bankB[bk]])
                if last:
                    copy_op("dve", yT[par][hh * 64:(hh + 1) * 64, 4 + j, :], bank[bk][hh * 64:(hh + 1) * 64, :],
                            reads=[bankB[bk]], writes=[yTB[par][4 + j]])

            def step(k):
                ok = lambda u: 0 <= u < n_u
                if ok(k - 1):
                    S2(k - 1)
                if ok(k + 1):
                    E(k + 1)
                if ok(k):
                    L(k)
                if ok(k - 1):
                    A(k - 1)
                if ok(k - 2):
                    S3(k - 2)
                if ok(k + 2):
                    QK(k + 2)

            return [(lambda k=k: step(k)) for k in range(-2, n_u + 3)]

        def interleave(steps, dense, total_w):
            n = len(steps)
            emitted = 0.0
            alive = dense is not None
            for k, fn in enumerate(steps):
                fn()
                target = total_w * (k + 1) / n
                while alive and emitted < target:
                    try:
                        emitted += next(dense)
                    except StopIteration:
                        alive = False
            if alive:
                for _ in dense:
                    pass

        def chain_gens(*gens):
            for g in gens:
                if g is not None:
                    yield from g

        def mix(g1, W1, g2, W2, scale2=0.3):
            d1 = d2 = 0.0
            a1 = g1 is not None
            a2 = g2 is not None
            while a1 or a2:
                pick1 = a1 and (not a2 or d1 / W1 <= d2 / W2)
                if pick1:
                    try:
                        w = next(g1)
                        d1 += w
                        yield w
                    except StopIteration:
                        a1 = False
                else:
                    try:
                        w = next(g2)
                        d2 += w
                        yield w * scale2
                    except StopIteration:
                        a2 = False

        for n in range(NTOT):
            l, i = divmod(n, NT)
            if i == 0:
                P.epoch = l
                load_params(l)
                for _ in gen_Ah(l, 0):
                    pass
                flush()
            gC = gen_C((n - 1) // NT, (n - 1) % NT) if n > 0 else None
            gAt = gen_At(l, i)
            gAh = gen_Ah(l, i + 1) if i + 1 < NT else None
            tw = (WC_TOTAL if n > 0 else 0) + 0.3 * WAT + (WAH if i + 1 < NT else 0)
            interleave(att_steps(l, i), chain_gens(mix(gC, WC_TOTAL, gAt, WAT), gAh), tw)
            flush()
        for _ in gen_C(DEPTH - 1, NT - 1):
            pass
        flush()
        for b in range(4):
            P.wait("sp", ("dma", s_st[b]["name"], s_st[b]["cnt"]))
        assert wk[0] == len(wseq), (wk[0], len(wseq))
        P.emit(nc, DEPTH)
    return nc


def make_consts():
    bf = ml_dtypes.bfloat16
    j = np.arange(128)[:, None]
    s = np.arange(128)[None, :]
    return {
        "c_ident_bf": np.eye(128).astype(bf),
        "c_ident_f": np.eye(128).astype(np.float32),
        "c_negtri": np.where(j >= s, -1.0, 0.0).astype(bf),
        "c_negones": np.full((128, 128), -1.0).astype(bf),
        "c_cmask": np.where(j < s, 1.0, 0.0).astype(bf),
        "c_avg": np.full((128, 128), 1.0 / 256.0, dtype=np.float32),
    }


_CACHE = {}


def kernel(x, g_attn, w_in, w_conv_a, w_conv_b, b_conv_b, ln_b_g, ln_b_b, w_out, g_ffn, w_ff1, w_ff2, g_final):
    x = np.asarray(x)
    B, T, _ = x.shape
    DEPTH = np.asarray(g_attn).shape[0]
    key = (T, DEPTH)
    if key not in _CACHE:
        _CACHE[key] = build(T, DEPTH)
    nc = _CACHE[key]
    f = lambda a: np.ascontiguousarray(np.asarray(a, dtype=np.float32))
    shared = {
        "g_attn": f(g_attn), "w_in": f(w_in), "w_conv_a": f(w_conv_a), "w_conv_b": f(w_conv_b),
        "b_conv_b": f(b_conv_b), "ln_b_g": f(ln_b_g), "ln_b_b": f(ln_b_b), "w_out": f(w_out),
        "g_ffn": f(g_ffn), "w_ff1": f(w_ff1), "w_ff2": f(w_ff2), "g_final": f(g_final).reshape(1, D),
    }
    shared.update(make_consts())
    in_maps = []
    for b in range(B):
        m = dict(shared)
        m["x"] = f(x[b])
        in_maps.append(m)
    res = run_bass_kernel_spmd(nc, in_maps, core_ids=list(range(B)))
    return np.stack([np.asarray(r["out"]) for r in res.results], axis=0).astype(np.float32)
```

```python
import contextlib
import numpy as np
import ml_dtypes
import concourse.bass as bass
import concourse.mybir as mybir
from concourse.bass_utils import run_bass_kernel_spmd

F32 = mybir.dt.float32
BF16 = mybir.dt.bfloat16
AF = mybir.ActivationFunctionType
ALU = mybir.AluOpType

D = 1024
NCOL = 2816
DFF = 4096
RMS_EPS = 1e-6
LN_EPS = 1e-5
NS = 4
IN_SLOTS = [(0, 4), (4, 8), (8, 12), (12, 16), (16, 18), (18, 22)]


class Buf:
    __slots__ = ("w", "r")

    def __init__(self):
        self.w = None
        self.r = {}


class Prog:
    ENGS = ("pe", "act", "dve", "pool", "sp")

    def __init__(self):
        self.items = {e: [] for e in self.ENGS}
        self.nops = {e: 0 for e in self.ENGS}
        self.signaled = {e: set() for e in self.ENGS}
        self.epoch = 0
        self.op_epoch = {e: [] for e in self.ENGS}
        self.dma_sems = []
        self.waited = {e: {} for e in self.ENGS}

    def op(self, eng, fn, waits=()):
        for w in waits:
            self.wait(eng, w)
        idx = self.nops[eng]
        self.nops[eng] += 1
        self.op_epoch[eng].append(self.epoch)
        self.items[eng].append(("op", fn, idx))
        return (eng, idx)

    def wait(self, eng, tok):
        if tok is None:
            return
        if tok[0] == "dma":
            _, sname, cnt = tok
            key = ("dma", sname)
            if self.waited[eng].get(key, -1) >= cnt:
                return
            self.waited[eng][key] = cnt
            self.items[eng].append(("wdma", sname, cnt))
            return
        src, idx = tok
        key = (src, self.op_epoch[src][idx])
        if self.waited[eng].get(key, -1) >= idx:
            return
        self.waited[eng][key] = idx
        self.signaled[src].add(idx)
        self.items[eng].append(("wait", src, idx))

    def new_dma_sem(self, name):
        self.dma_sems.append(name)
        return {"name": name, "cnt": 0}

    def dma(self, eng, sem, fn, waits=()):
        for w in waits:
            self.wait(eng, w)
        sem["cnt"] += 16
        self.items[eng].append(("dma", fn, sem["name"]))
        return ("dma", sem["name"], sem["cnt"])

    def use(self, eng, fn, reads=(), writes=(), dma_sem=None, nosame=False):
        waits = []
        for b in reads:
            if b.w is not None:
                waits.append(b.w)
        for b in writes:
            if b.w is not None:
                waits.append(b.w)
            for t in b.r.values():
                waits.append(t)
        fw = []
        for w in waits:
            if w[0] != "dma" and w[0] == eng and (eng == "pe" or nosame):
                continue
            fw.append(w)
        if dma_sem is not None:
            tok = self.dma(eng, dma_sem, fn, waits=fw)
        else:
            tok = self.op(eng, fn, waits=fw)
        key = ("dma", tok[1]) if tok[0] == "dma" else tok[0]
        for b in reads:
            b.r[key] = tok
        for b in writes:
            b.w = tok
            b.r = {}
        return tok

    def emit(self, nc, n_epochs):
        counts = {}
        for e in self.ENGS:
            c = {}
            for idx in range(self.nops[e]):
                if idx in self.signaled[e]:
                    ep = self.op_epoch[e][idx]
                    c[ep] = c.get(ep, 0) + 1
                    counts[(e, idx)] = c[ep]
        with contextlib.ExitStack() as st:
            sems = {}
            for e in self.ENGS:
                for ep in range(n_epochs):
                    sems[(e, ep)] = st.enter_context(nc.semaphore(f"s_{e}_{ep}"))
            dsems = {n: st.enter_context(nc.semaphore(f"d_{n}")) for n in self.dma_sems}
            block = st.enter_context(nc.Block())

            def run(engname):
                def body(eng):
                    for it in self.items[engname]:
                        k = it[0]
                        if k == "op":
                            ins = it[1](eng)
                            idx = it[2]
                            if idx in self.signaled[engname]:
                                ins.then_inc(sems[(engname, self.op_epoch[engname][idx])], 1)
                        elif k == "wait":
                            _, src, idx = it
                            eng.wait_ge(sems[(src, self.op_epoch[src][idx])], counts[(src, idx)])
                        elif k == "wdma":
                            eng.wait_ge(dsems[it[1]], it[2])
                        else:
                            it[1](eng).then_inc(dsems[it[2]], 16)
                return body

            block.tensor(run("pe"))
            block.scalar(run("act"))
            block.vector(run("dve"))
            block.gpsimd(run("pool"))
            block.sync(run("sp"))


def build(T, DEPTH):
    NT = T // 512
    NB = T // 128
    nc = bass.Bass("TRN2", target_bir_lowering=False)
    dt = lambda n, s, d, k: nc.dram_tensor(n, s, d, kind=k).ap()
    x_in = dt("x", [T, D], F32, "ExternalInput")
    g_attn = dt("g_attn", [DEPTH, D], F32, "ExternalInput")
    w_in = dt("w_in", [DEPTH, D, NCOL], F32, "ExternalInput")
    w_conv_a = dt("w_conv_a", [DEPTH, 3, 256], F32, "ExternalInput")
    w_conv_b = dt("w_conv_b", [DEPTH, 31, 256], F32, "ExternalInput")
    b_conv_b = dt("b_conv_b", [DEPTH, 256], F32, "ExternalInput")
    ln_b_g = dt("ln_b_g", [DEPTH, 256], F32, "ExternalInput")
    ln_b_b = dt("ln_b_b", [DEPTH, 256], F32, "ExternalInput")
    w_out = dt("w_out", [DEPTH, D, D], F32, "ExternalInput")
    g_ffn = dt("g_ffn", [DEPTH, D], F32, "ExternalInput")
    w_ff1 = dt("w_ff1", [DEPTH, D, DFF], F32, "ExternalInput")
    w_ff2 = dt("w_ff2", [DEPTH, DFF, D], F32, "ExternalInput")
    g_final = dt("g_final", [1, D], F32, "ExternalInput")
    c_ident_bf = dt("c_ident_bf", [128, 128], BF16, "ExternalInput")
    c_ident_f = dt("c_ident_f", [128, 128], F32, "ExternalInput")
    c_negtri = dt("c_negtri", [128, 128], BF16, "ExternalInput")
    c_negones = dt("c_negones", [128, 128], BF16, "ExternalInput")
    c_cmask = dt("c_cmask", [128, 128], BF16, "ExternalInput")
    c_avg = dt("c_avg", [128, 128], F32, "ExternalInput")
    out = dt("out", [T, D], F32, "ExternalOutput")
    xres = dt("xres", [T, D], F32, "Internal")

    P = Prog()
    with contextlib.ExitStack() as st:
        sb = lambda n, s, d: st.enter_context(nc.sbuf_tensor(n, s, d))
        pst = lambda n, s, d: st.enter_context(nc.psum_tensor(n, s, d))

        ring = [sb(f"ring{s}", [128, 4096], BF16) for s in range(NS)]
        ringB = [Buf() for _ in range(NS)]
        kT = sb("kT", [128, 4, T], BF16)
        kTB = [[Buf() for _ in range(NT)] for _ in range(4)]
        Vc = sb("Vc", [128, NB, 512], BF16)
        VcB = [Buf() for _ in range(NB)]
        xt = sb("xt", [128, 4, D], F32)
        xtB = [Buf() for _ in range(4)]
        hn2 = [sb(f"hn{k}", [128, D], BF16) for k in range(2)]
        hn2B = [Buf(), Buf()]
        hT = sb("hT", [128, 8, 512], BF16)
        hTB = Buf()
        qm = [sb(f"qm{k}", [128, 4, 2, 512], BF16) for k in range(2)]
        qmB = [[Buf() for _ in range(4)] for _ in range(2)]
        yT = [sb(f"yT{k}", [128, 8, 512], BF16) for k in range(2)]
        yTB = [[Buf() for _ in range(8)] for _ in range(2)]
        uT = [sb(f"uT{k}", [128, 4, 512], BF16) for k in range(2)]
        uTB = [Buf(), Buf()]
        NSCR = 5
        scr = [sb(f"scr{k}", [128, 512], F32) for k in range(NSCR)]
        scrB = [Buf() for _ in range(NSCR)]
        mbuf = sb("mbuf", [128, 2, 514], F32)
        mbufB = [Buf(), Buf()]
        ubuf = sb("ubuf", [128, 2, 542], F32)
        ubufB = [Buf(), Buf()]
        gcv = [sb(f"gcv{k}", [128, 512], F32) for k in range(2)]
        gcvB = [Buf(), Buf()]
        Eb = [sb(f"Eb{k}", [128, 512], F32) for k in range(2)]
        EbB = [Buf(), Buf()]
        Lb = [sb(f"Lb{k}", [128, 512], BF16) for k in range(2)]
        LbB = [Buf(), Buf()]
        Ab = [sb(f"Ab{k}", [128, 512], BF16) for k in range(2)]
        AbB = [Buf(), Buf()]
        Ls = [sb(f"Ls{k}", [128, 512], BF16) for k in range(2)]
        LsB = [Buf(), Buf()]
        ident_bf = sb("ident_bf", [128, 128], BF16)
        ident_f = sb("ident_f", [128, 128], F32)
        negtri = sb("negtri", [128, 128], BF16)
        negones = sb("negones", [128, 128], BF16)
        cmask = sb("cmask", [128, 128], BF16)
        avgm = sb("avgm", [128, 128], F32)
        constB = Buf()
        gFin = sb("gFin", [128, D], F32)
        gFinB = Buf()
        prm = sb("prm", [37, 256], F32)
        prmB = Buf()
        gst = sb("gst", [8, 2, 128], F32)
        gstB = Buf()
        prmT = sb("prmT", [128, 2, 2, 37], F32)
        prmTB2 = [Buf(), Buf()]
        gT = sb("gT", [128, 2, 2, 8], F32)
        gTB2 = [Buf(), Buf()]
        stat = sb("stat", [128, 16], F32)
        statB = Buf()

        bank = [pst(f"bank{k}", [128, 512], F32) for k in range(5)]
        pT = pst("pT", [128, D], BF16)
        bank += [None, pst("bank6", [128, 512], F32), pst("bank7", [128, 512], F32)]
        bankB = [Buf() for _ in range(8)]
        pTB = bankB[5]
        PSA = (0, 1)
        PSB = (2, 3)
        PSO = 4
        DN = (6, 7)

        s_ring = [P.new_dma_sem(f"ring{s}") for s in range(NS)]
        s_x = [P.new_dma_sem(f"x{b}") for b in range(4)]
        s_st = [P.new_dma_sem(f"st{b}") for b in range(4)]
        s_c = P.new_dma_sem("c")
        s_gf = P.new_dma_sem("gf")
        s_gs = P.new_dma_sem("gs")
        s_p = P.new_dma_sem("p")

        xresB = [[Buf() for _ in range(4)] for _ in range(NT)]

        NTOT = DEPTH * NT
        dorder = []
        for n in range(NTOT):
            l_, i_ = divmod(n, NT)
            if i_ == 0:
                dorder.append(("A", l_))
            if n > 0:
                dorder.append(("C", (n - 1) // NT))
            if (n + 1) % NT != 0:
                dorder.append(("A", l_))
        dorder.append(("C", DEPTH - 1))
        wseq = []
        for ph, l in dorder:
            if ph == "A":
                for g in range(6):
                    wseq.append(("in", l, g))
            else:
                for eh in range(2):
                    wseq.append(("out", l, eh))
                order = [("w1", 0), ("w1", 1)]
                for g in range(2, 8):
                    order += [("w2", g - 2), ("w1", g)]
                order += [("w2", 6), ("w2", 7)]
                for kind, g in order:
                    wseq.append((kind, l, g))
        wstate = {"loaded": 0}

        def author_load(k):
            kind, l, g = wseq[k]
            s = k % NS
            slot = ring[s]
            if kind == "in":
                lo, hi = IN_SLOTS[g]
                n = (hi - lo) * 128
                dst = slot[:].rearrange("p (c n) -> p c n", c=8)[:, :, 0:n]
                src = w_in[l].rearrange("(c p) n -> p c n", p=128)[:, :, lo * 128:hi * 128]
            elif kind == "out":
                dst = slot[:].rearrange("p (c n) -> p c n", c=8)
                src = w_out[l].rearrange("(c p) n -> p c n", p=128)[:, :, g * 512:(g + 1) * 512]
            elif kind == "w1":
                dst = slot[:].rearrange("p (c n) -> p c n", c=8)
                src = w_ff1[l].rearrange("(c p) n -> p c n", p=128)[:, :, g * 512:(g + 1) * 512]
            else:
                dst = slot[:].rearrange("p (c n) -> p c n", c=4)
                src = w_ff2[l][g * 512:(g + 1) * 512, :].rearrange("(c p) n -> p c n", p=128)
            P.use("pool", lambda e, dst=dst, src=src: e.dma_start(out=dst, in_=src),
                  writes=[ringB[s]], dma_sem=s_ring[s])

        def get_w(k, kind=None):
            while wstate["loaded"] < min(len(wseq), NS):
                author_load(wstate["loaded"])
                wstate["loaded"] += 1
            assert wstate["loaded"] > k, (k, wstate)
            if kind is not None:
                assert wseq[k][0] == kind[0] and wseq[k][2] == kind[1], (wseq[k], kind)
            return k % NS

        def release(k):
            nk = k + NS
            if nk < len(wseq):
                assert wstate["loaded"] == nk, (k, wstate)
                author_load(nk)
                wstate["loaded"] += 1

        for dst_t, src in ((ident_bf, c_ident_bf), (ident_f, c_ident_f), (negtri, c_negtri),
                           (negones, c_negones), (cmask, c_cmask), (avgm, c_avg)):
            P.use("sp", lambda e, d=dst_t, s_=src: e.dma_start(out=d[:], in_=s_), writes=[constB], dma_sem=s_c)
        P.use("sp", lambda e: e.dma_start(out=gFin[:], in_=g_final.partition_broadcast(128)), writes=[gFinB], dma_sem=s_gf)
        for k in range(2):
            for j in range(4):
                P.use("dve", lambda e, j=j, k=k: e.memset(qm[k][:, j, :, :], 0.0), writes=[qmB[k][j]])

        dctr = {"d": 0, "ev": 0}
        pend = []

        def flush():
            while pend:
                pend.pop(0)()

        def next_bank():
            b = DN[dctr["d"] % 2]
            dctr["d"] += 1
            return b

        def evac_eng():
            dctr["ev"] += 1
            return "act" if dctr["ev"] % 2 else "dve"

        def copy_op(eng, out_ap, in_ap, reads, writes):
            if eng == "act":
                return P.use("act", lambda e: e.activation(out=out_ap, in_=in_ap, func=AF.Copy), reads=reads, writes=writes)
            return P.use("dve", lambda e: e.tensor_copy(out=out_ap, in_=in_ap), reads=reads, writes=writes)

        def rms_stats(b):
            c0 = 3 * b
            P.use("dve", lambda e: e.memset(stat[:, c0:c0 + 1], 0.0), writes=[statB])
            P.use("act", lambda e: e.activation(out=hn2[b % 2][:], in_=xt[:, b, :], func=AF.Square, accum_out=stat[:, c0:c0 + 1]),
                  reads=[xtB[b]], writes=[statB, hn2B[b % 2]])
            P.use("act", lambda e: e.activation(out=stat[:, c0 + 1:c0 + 2], in_=stat[:, c0:c0 + 1], func=AF.Ln,
                                                scale=1.0 / D, bias=RMS_EPS), reads=[statB], writes=[statB])
            P.use("act", lambda e: e.activation(out=stat[:, c0 + 2:c0 + 3], in_=stat[:, c0 + 1:c0 + 2], func=AF.Exp,
                                                scale=-0.5), reads=[statB], writes=[statB])
            return stat[:, c0 + 2:c0 + 3]

        def norm_gen(lp, gk):
            gTB = gTB2[lp]
            rs = []
            for b in range(4):
                rs.append(rms_stats(b))
                yield 4
            gb = gT[:, lp, gk, :].unsqueeze(2).to_broadcast([128, 8, 128])
            for b in range(4):
                hn, hnB = hn2[b % 2], hn2B[b % 2]
                P.use("dve", lambda e, b=b, r=rs[b], hn=hn: e.tensor_scalar_mul(out=hn[:], in0=xt[:, b, :], scalar1=r),
                      reads=[xtB[b], statB], writes=[hnB])
                for c in range(8):
                    P.use("pe", lambda e, c=c, hn=hn: e.transpose(out=pT[:, c * 128:(c + 1) * 128],
                                                                  in_=hn[:, c * 128:(c + 1) * 128], identity=ident_bf[:]),
                          reads=[hnB, constB], writes=[pTB])
                yield 6
                P.use("dve", lambda e, b=b: e.tensor_tensor(out=hT[:, :, b * 128:(b + 1) * 128],
                                                            in0=pT[:].rearrange("p (c n) -> p c n", c=8), in1=gb, op=ALU.mult),
                      reads=[pTB, gTB], writes=[hTB])
                yield 6

        wk = [0]

        def load_x(l, i):
            src = x_in if l == 0 else xres
            for b in range(4):
                rd = [] if l == 0 else [xresB[i][b]]
                r0 = i * 512 + b * 128
                P.use("sp", lambda e, b=b, r0=r0: e.dma_start(out=xt[:, b, :], in_=src[r0:r0 + 128, :]),
                      reads=rd, writes=[xtB[b]], dma_sem=s_x[b])

        def load_params(l):
            lp = l % 2
            prmTB, gTB = prmTB2[lp], gTB2[lp]
            for (r0, r1, src) in ((0, 3, w_conv_a[l]), (3, 34, w_conv_b[l]), (34, 35, b_conv_b[l:l + 1, :]),
                                  (35, 36, ln_b_g[l:l + 1, :]), (36, 37, ln_b_b[l:l + 1, :])):
                P.use("sp", lambda e, r0=r0, r1=r1, src=src: e.dma_start(out=prm[r0:r1, :], in_=src),
                      writes=[prmB], dma_sem=s_p)
            for k, gsrc in enumerate((g_attn, g_ffn)):
                P.use("sp", lambda e, k=k, gsrc=gsrc: e.dma_start(
                    out=gst[:, k, :], in_=gsrc[l:l + 1, :].rearrange("o (c p) -> (o c) p", p=128)),
                    writes=[gstB], dma_sem=s_gs)
            for c in range(2):
                bk = next_bank()
                P.use("pe", lambda e, c=c, bk=bk: e.transpose(out=bank[bk][:, 0:37], in_=prm[0:37, c * 128:(c + 1) * 128],
                                                              identity=ident_f[0:37, 0:37]),
                      reads=[prmB, constB], writes=[bankB[bk]])
                P.use("dve", lambda e, c=c, bk=bk: e.tensor_copy(out=prmT[:, lp, c, :], in_=bank[bk][:, 0:37]),
                      reads=[bankB[bk]], writes=[prmTB])
            for k in range(2):
                bk = next_bank()
                P.use("pe", lambda e, k=k, bk=bk: e.transpose(out=bank[bk][:, 0:8], in_=gst[0:8, k, :],
                                                              identity=ident_f[0:8, 0:8]),
                      reads=[gstB, constB], writes=[bankB[bk]])
                P.use("dve", lambda e, k=k, bk=bk: e.tensor_copy(out=gT[:, lp, k, :], in_=bank[bk][:, 0:8]),
                      reads=[bankB[bk]], writes=[gTB])
            for c in range(2):
                P.use("dve", lambda e, c=c: e.memset(mbuf[:, c, 0:2], 0.0), writes=[mbufB[c]])
                P.use("dve", lambda e, c=c: e.memset(ubuf[:, c, 0:30], 0.0), writes=[ubufB[c]])


        def gen_Ah(l, i, act_ok=True):
            ceng = "act" if act_ok else "dve"
            par = i % 2
            lp = l % 2
            prmTB = prmTB2[lp]
            pv = lambda c, r: prmT[:, lp, c, r:r + 1]
            load_x(l, i)
            for _ in range(4):
                yield 8
            for w in norm_gen(l % 2, 0):
                yield w
            wbase = wk[0]
            wk[0] += 6

            def mm_chunk(cc):
                g = [k for k, (lo, hi) in enumerate(IN_SLOTS) if lo <= cc < hi][0]
                off = cc - IN_SLOTS[g][0]
                s = get_w(wbase + g, ("in", g))
                wv = ring[s][:].rearrange("p (c n) -> p c n", c=8)
                bk = next_bank()
                for c in range(8):
                    P.use("pe", lambda e, c=c: e.matmul(
                        bank[bk][:], lhsT=wv[:, c, off * 128:(off + 1) * 128], rhs=hT[:, c, :],
                        start=(c == 0), stop=(c == 7)),
                        reads=[ringB[s], hTB], writes=[bankB[bk]])
                return bk

            for c in range(2):
                bk = mm_chunk(2 + c)
                flush()
                pend.append(lambda bk=bk: P.use("act", lambda e: e.activation(out=scr[0][:], in_=bank[bk][:], func=AF.Copy),
                                                reads=[bankB[bk]], writes=[scrB[0]]))
                yield 8
                bk = mm_chunk(4 + c)
                flush()

                def ev_ah(bk=bk, c=c):
                    P.use("dve", lambda e: e.tensor_tensor(out=mbuf[:, c, 2:514], in0=bank[bk][:], in1=scr[0][:], op=ALU.mult),
                          reads=[bankB[bk], scrB[0]], writes=[mbufB[c]])
                    P.use("dve", lambda e: e.tensor_scalar_mul(out=scr[1][:], in0=mbuf[:, c, 0:512], scalar1=pv(c, 0)),
                          reads=[mbufB[c], prmTB], writes=[scrB[1]])
                    P.use("dve", lambda e: e.scalar_tensor_tensor(out=scr[2][:], in0=mbuf[:, c, 1:513], scalar=pv(c, 1), in1=scr[1][:],
                                                                  op0=ALU.mult, op1=ALU.add),
                          reads=[mbufB[c], scrB[1]], writes=[scrB[2]])
                    P.use("dve", lambda e: e.scalar_tensor_tensor(out=scr[1][:], in0=mbuf[:, c, 2:514], scalar=pv(c, 2), in1=scr[2][:],
                                                                  op0=ALU.mult, op1=ALU.add),
                          reads=[mbufB[c], scrB[2]], writes=[scrB[1]])
                pend.append(ev_ah)
                yield 8
                bk = mm_chunk(c)
                flush()

                def ev_ab(bk=bk, c=c):
                    P.use("dve", lambda e: e.tensor_tensor(out=yT[par][:, c, :], in0=bank[bk][:], in1=scr[1][:], op=ALU.mult),
                          reads=[bankB[bk], scrB[1]], writes=[yTB[par][c]])
                    P.use("dve", lambda e: e.tensor_copy(out=mbuf[:, c, 0:2], in_=mbuf[:, c, 512:514]),
                          reads=[mbufB[c]], writes=[mbufB[c]])
                pend.append(ev_ab)
                yield 8
            release(wbase + 0)
            for c in range(2):
                bk = mm_chunk(6 + c)
                flush()
                pend.append(lambda bk=bk: P.use("act", lambda e: e.activation(out=scr[0][:], in_=bank[bk][:], func=AF.Copy),
                                                reads=[bankB[bk]], writes=[scrB[0]]))
                yield 8
                bk = mm_chunk(8 + c)
                flush()

                def ev_gate(bk=bk, c=c):
                    P.use("act", lambda e: e.activation(out=scr[1][:], in_=bank[bk][:], func=AF.Exp, scale=-1.0),
                          reads=[bankB[bk]], writes=[scrB[1]])
                    P.use("dve", lambda e: e.tensor_scalar_add(out=scr[2][:], in0=scr[1][:], scalar1=1.0),
                          reads=[scrB[1]], writes=[scrB[2]])
                    P.use("dve", lambda e: e.reciprocal(out=scr[1][:], in_=scr[2][:]), reads=[scrB[2]], writes=[scrB[1]])
                    P.use("dve", lambda e: e.tensor_tensor(out=ubuf[:, c, 30:542], in0=scr[0][:], in1=scr[1][:], op=ALU.mult),
                          reads=[scrB[0], scrB[1]], writes=[ubufB[c]])
                pend.append(ev_gate)
                yield 8
            release(wbase + 1)
            flush()
            for j in range(4):
                bk = mm_chunk(10 + j)
                flush()

                def ev_q(bk=bk, j=j):
                    for hh in range(2):
                        P.use("act", lambda e, hh=hh: e.mul(out=qm[par][hh * 64:(hh + 1) * 64, j, hh, :],
                                                            in_=bank[bk][hh * 64:(hh + 1) * 64, :], mul=0.125),
                              reads=[bankB[bk]], writes=[qmB[par][j]])
                pend.append(ev_q)
                if j == 1:
                    release(wbase + 2)
                yield 8
            for j in range(4):
                bk = mm_chunk(14 + j)
                flush()
                pend.append(lambda bk=bk, j=j: copy_op(ceng, kT[:, j, i * 512:(i + 1) * 512], bank[bk][:],
                                                       reads=[bankB[bk]], writes=[kTB[j][i]]))
                if j == 1:
                    release(wbase + 3)
                if j == 3:
                    release(wbase + 4)
                yield 8
            s5 = get_w(wbase + 5, ("in", 5))
            w5 = ring[s5][:].rearrange("p (c n) -> p c n", c=8)
            for b in range(4):
                bk = next_bank()
                for c in range(8):
                    P.use("pe", lambda e, c=c, bk=bk, b=b: e.matmul(
                        bank[bk][:], lhsT=hT[:, c, b * 128:(b + 1) * 128], rhs=w5[:, c, :],
                        start=(c == 0), stop=(c == 7)), reads=[ringB[s5], hTB], writes=[bankB[bk]])
                flush()
                pend.append(lambda bk=bk, b=b: copy_op(ceng, Vc[:, 4 * i + b, :], bank[bk][:],
                                                       reads=[bankB[bk]], writes=[VcB[4 * i + b]]))
                yield 8
            release(wbase + 5)
            flush()
            yield 1

        def gen_At(l, i):
            par = i % 2
            lp = l % 2
            prmTB = prmTB2[lp]
            pv = lambda c, r: prmT[:, lp, c, r:r + 1]
            chain = []
            for c in range(2):
                chain.append(lambda c=c: P.use("dve", lambda e: e.tensor_scalar(
                    out=gcv[c][:], in0=ubuf[:, c, 0:512], scalar1=pv(c, 3), scalar2=pv(c, 34), op0=ALU.mult, op1=ALU.add),
                    reads=[ubufB[c], prmTB], writes=[gcvB[c]]))
            for k in range(1, 31):
                for c in range(2):
                    chain.append(lambda c=c, k=k: P.use("dve", lambda e: e.scalar_tensor_tensor(
                        out=gcv[c][:], in0=ubuf[:, c, k:k + 512], scalar=pv(c, 3 + k), in1=gcv[c][:], op0=ALU.mult, op1=ALU.add),
                        reads=[ubufB[c], gcvB[c]], writes=[gcvB[c]]))
            for c in range(2):
                chain.append(lambda c=c: P.use("dve", lambda e: e.tensor_copy(out=ubuf[:, c, 0:30], in_=ubuf[:, c, 512:542]),
                                               reads=[ubufB[c]], writes=[ubufB[c]]))

            def do_chain(n):
                for _ in range(n):
                    if chain:
                        chain.pop(0)()

            while chain:
                do_chain(4)
                yield 13
            flush()
            bm = next_bank()
            for c in range(2):
                P.use("pe", lambda e, c=c: e.matmul(bank[bm][:], lhsT=avgm[:], rhs=gcv[c][:], start=(c == 0), stop=(c == 1)),
                      reads=[gcvB[c], constB], writes=[bankB[bm]])
            for c in range(2):
                P.use("dve", lambda e, c=c: e.tensor_tensor(out=gcv[c][:], in0=gcv[c][:], in1=bank[bm][:], op=ALU.subtract),
                      reads=[gcvB[c], bankB[bm]], writes=[gcvB[c]])
                P.use("dve", lambda e, c=c: e.tensor_tensor(out=scr[3 + c][:], in0=gcv[c][:], in1=gcv[c][:], op=ALU.mult),
                      reads=[gcvB[c]], writes=[scrB[3 + c]])
            yield 8
            yield 4
            flush()
            bv = next_bank()
            for c in range(2):
                P.use("pe", lambda e, c=c: e.matmul(bank[bv][:], lhsT=avgm[:], rhs=scr[3 + c][:], start=(c == 0), stop=(c == 1)),
                      reads=[scrB[3 + c], constB], writes=[bankB[bv]])
            P.use("act", lambda e: e.activation(out=scr[1][:], in_=bank[bv][:], func=AF.Ln, bias=LN_EPS),
                  reads=[bankB[bv]], writes=[scrB[1]])
            P.use("act", lambda e: e.activation(out=scr[0][:], in_=scr[1][:], func=AF.Exp, scale=-0.5),
                  reads=[scrB[1]], writes=[scrB[0]])
            yield 8
            for c in range(2):
                P.use("dve", lambda e, c=c: e.tensor_tensor(out=gcv[c][:], in0=gcv[c][:], in1=scr[0][:], op=ALU.mult),
                      reads=[gcvB[c], scrB[0]], writes=[gcvB[c]])
                P.use("dve", lambda e, c=c: e.tensor_scalar(out=gcv[c][:], in0=gcv[c][:], scalar1=pv(c, 35), scalar2=pv(c, 36),
                                                             op0=ALU.mult, op1=ALU.add),
                      reads=[gcvB[c], prmTB], writes=[gcvB[c]])
                P.use("act", lambda e, c=c: e.activation(out=scr[1 + c][:], in_=gcv[c][:], func=AF.Exp, scale=-1.0),
                      reads=[gcvB[c]], writes=[scrB[1 + c]])
                P.use("dve", lambda e, c=c: e.tensor_scalar_add(out=scr[3 + c][:], in0=scr[1 + c][:], scalar1=1.0),
                      reads=[scrB[1 + c]], writes=[scrB[3 + c]])
                P.use("dve", lambda e, c=c: e.reciprocal(out=scr[1 + c][:], in_=scr[3 + c][:]), reads=[scrB[3 + c]], writes=[scrB[1 + c]])
                P.use("dve", lambda e, c=c: e.tensor_tensor(out=yT[par][:, 2 + c, :], in0=gcv[c][:], in1=scr[1 + c][:], op=ALU.mult),
                      reads=[gcvB[c], scrB[1 + c]], writes=[yTB[par][2 + c]])
                yield 2

        WAH = 6 + 56 + 48 + 32 + 96 + 1
        WAT = 17 * 13 + 8 + 4 + 8 + 4

        def gen_C(l, i, act_ok=True):
            par = i % 2
            load_x(l, i)
            yield 3

            def add_x(bk, b, eh):
                P.use("dve", lambda e: e.tensor_tensor(
                    out=xt[:, b, eh * 512:(eh + 1) * 512], in0=bank[bk][:], in1=xt[:, b, eh * 512:(eh + 1) * 512], op=ALU.add),
                    reads=[bankB[bk], xtB[b]], writes=[xtB[b]])

            for eh in range(2):
                s = get_w(wk[0], ("out", eh))
                wv = ring[s][:].rearrange("p (c n) -> p c n", c=8)
                for b in range(4):
                    bk = next_bank()
                    for c in range(8):
                        P.use("pe", lambda e, c=c, bk=bk, b=b, wv=wv: e.matmul(
                            bank[bk][:], lhsT=yT[par][:, c, b * 128:(b + 1) * 128], rhs=wv[:, c, :],
                            start=(c == 0), stop=(c == 7)), reads=[ringB[s], yTB[par][c]], writes=[bankB[bk]])
                    flush()
                    pend.append(lambda bk=bk, b=b, eh=eh: add_x(bk, b, eh))
                    yield 8
                release(wk[0])
                wk[0] += 1
            flush()
            for w in norm_gen(l % 2, 1):
                yield w

            def ffn1(g):
                s = get_w(wk[0], ("w1", g))
                wv = ring[s][:].rearrange("p (c n) -> p c n", c=8)
                ub = g % 2
                for f4 in range(4):
                    bk = next_bank()
                    for c in range(8):
                        P.use("pe", lambda e, c=c, bk=bk, f4=f4: e.matmul(
                            bank[bk][:], lhsT=wv[:, c, f4 * 128:(f4 + 1) * 128], rhs=hT[:, c, :],
                            start=(c == 0), stop=(c == 7)), reads=[ringB[s], hTB], writes=[bankB[bk]])
                    flush()

                    def ev(bk=bk, f4=f4):
                        if act_ok:
                            P.use("act", lambda e: e.activation(out=uT[ub][:, f4, :], in_=bank[bk][:], func=AF.Relu),
                                  reads=[bankB[bk]], writes=[uTB[ub]])
                        else:
                            P.use("dve", lambda e: e.tensor_scalar_max(out=uT[ub][:, f4, :], in0=bank[bk][:], scalar1=0.0),
                                  reads=[bankB[bk]], writes=[uTB[ub]])
                        P.use("dve", lambda e: e.tensor_tensor(out=uT[ub][:, f4, :], in0=bank[bk][:], in1=uT[ub][:, f4, :], op=ALU.mult),
                              reads=[bankB[bk], uTB[ub]], writes=[uTB[ub]])
                    pend.append(ev)
                    yield 8
                release(wk[0])
                wk[0] += 1

            def ffn2(g):
                s = get_w(wk[0], ("w2", g))
                wv2 = ring[s][:].rearrange("p (c n) -> p c n", c=4)
                ub = g % 2
                flush()
                for eh in range(2):
                    for b in range(4):
                        bk = next_bank()
                        for f4 in range(4):
                            P.use("pe", lambda e, f4=f4, bk=bk, b=b, eh=eh: e.matmul(
                                bank[bk][:], lhsT=uT[ub][:, f4, b * 128:(b + 1) * 128], rhs=wv2[:, f4, eh * 512:(eh + 1) * 512],
                                start=(f4 == 0), stop=(f4 == 3)), reads=[ringB[s], uTB[ub]], writes=[bankB[bk]])
                        flush()
                        pend.append(lambda bk=bk, b=b, eh=eh: add_x(bk, b, eh))
                        yield 4
                release(wk[0])
                wk[0] += 1

            yield from ffn1(0)
            yield from ffn1(1)
            for g in range(2, 8):
                yield from ffn2(g - 2)
                yield from ffn1(g)
            yield from ffn2(6)
            yield from ffn2(7)
            flush()
            if l == DEPTH - 1:
                rs = [rms_stats(b) for b in range(4)]
                for b in range(4):
                    P.use("dve", lambda e, b=b, r=rs[b]: e.scalar_tensor_tensor(
                        out=xt[:, b, :], in0=xt[:, b, :], scalar=r, in1=gFin[:], op0=ALU.mult, op1=ALU.mult),
                        reads=[xtB[b], statB, gFinB], writes=[xtB[b]])
                dst = out
            else:
                dst = xres
            for b in range(4):
                wr = [] if l == DEPTH - 1 else [xresB[i][b]]
                r0 = i * 512 + b * 128
                P.use("sp", lambda e, b=b, r0=r0: e.dma_start(out=dst[r0:r0 + 128, :], in_=xt[:, b, :]),
                      reads=[xtB[b]], writes=wr, dma_sem=s_st[b])
            yield 2

        WC_TOTAL = 3 + 64 + 56 + 256 + 256 + 2

        def att_steps(l, i):
            par = i % 2
            units = []
            for j in range(4):
                for hh in range(2):
                    for kb in range(4 * i + 3, -1, -1):
                        units.append((j, hh, kb))
            n_u = len(units)

            def geom(u):
                j, hh, kb = units[u]
                r = kb - 4 * i
                c0 = r * 128 if r >= 0 else 0
                return j, hh, kb, r, c0, (kb == 4 * i + 3), (kb == 0)

            PS = (0, 1, 2, 3)

            def QK(u):
                j, hh, kb, r, c0, first, last = geom(u)
                bk = PS[u % 4]
                P.use("pe", lambda e: e.matmul(bank[bk][:, c0:512], lhsT=kT[:, j, kb * 128:(kb + 1) * 128],
                                               rhs=qm[par][:, j, hh, c0:512], start=True, stop=False, skip_group_check=True),
                      reads=[kTB[j][kb // 4], qmB[par][j]], writes=[bankB[bk]])

            def E(u):
                j, hh, kb, r, c0, first, last = geom(u)
                bk = PS[u % 4]
                P.use("act", lambda e: e.activation(out=Eb[u % 2][:, c0:512], in_=bank[bk][:, c0:512], func=AF.Exp),
                      reads=[bankB[bk]], writes=[EbB[u % 2]])

            def L(u):
                j, hh, kb, r, c0, first, last = geom(u)
                P.use("act", lambda e: e.activation(out=Lb[u % 2][:, c0:512], in_=Eb[u % 2][:, c0:512], func=AF.Ln, bias=1.0),
                      reads=[EbB[u % 2]], writes=[LbB[u % 2]])
                if r >= 0:
                    P.use("dve", lambda e: e.tensor_tensor(out=Lb[u % 2][:, c0:c0 + 128], in0=Lb[u % 2][:, c0:c0 + 128],
                                                           in1=cmask[:], op=ALU.mult),
                          reads=[LbB[u % 2], constB], writes=[LbB[u % 2]])

            def S2(u):
                j, hh, kb, r, c0, first, last = geom(u)
                bk = PS[u % 4]
                lsi = (j * 2 + hh) % 2
                if first:
                    P.use("dve", lambda e: e.memset(Ls[lsi][:], 0.0), writes=[LsB[lsi]])
                P.use("pe", lambda e: e.matmul(bank[bk][:, c0:512], lhsT=negtri[:], rhs=Lb[u % 2][:, c0:512],
                                               start=False, stop=first, skip_group_check=True),
                      reads=[LbB[u % 2], constB, bankB[bk]], writes=[bankB[bk]])
                if not first:
                    P.use("pe", lambda e: e.matmul(bank[bk][:, c0:512], lhsT=negones[:], rhs=Ls[lsi][:, c0:512],
                                                   start=False, stop=True, skip_group_check=True),
                          reads=[LsB[lsi], constB], writes=[bankB[bk]])
                if not last:
                    P.use("dve", lambda e: e.tensor_tensor(out=Ls[lsi][:, c0:512], in0=Ls[lsi][:, c0:512],
                                                           in1=Lb[u % 2][:, c0:512], op=ALU.add),
                          reads=[LsB[lsi], LbB[u % 2]], writes=[LsB[lsi]])

            def A(u):
                j, hh, kb, r, c0, first, last = geom(u)
                bk = PS[u % 4]
                P.use("act", lambda e: e.activation(out=Ab[u % 2][:, c0:512], in_=bank[bk][:, c0:512], func=AF.Exp),
                      reads=[bankB[bk]], writes=[AbB[u % 2]])
                if r >= 0:
                    P.use("dve", lambda e: e.tensor_tensor(out=Ab[u % 2][:, c0:c0 + 128], in0=Ab[u % 2][:, c0:c0 + 128],
                                                           in1=cmask[:], op=ALU.mult),
                          reads=[AbB[u % 2], constB], writes=[AbB[u % 2]])

            def S3(u):
                j, hh, kb, r, c0, first, last = geom(u)
                bk = PSO
                vv = Vc[:, kb, j * 128:(j + 1) * 128]
                P.use("pe", lambda e: e.matmul(bank[bk][:, c0:512], lhsT=vv, rhs=Ab[u % 2][:, c0:512],
                                               start=first, stop=last, skip_group_check=True),
                      reads=[VcB[kb], AbB[u % 2]], writes=[bankB[bk]])
                if last:
                    copy_op("act" if i < 5 else "dve", yT[par][hh * 64:(hh + 1) * 64, 4 + j, :], bank[bk][hh * 64:(hh + 1) * 64, :],
                            reads=[bankB[bk]], writes=[yTB[par][4 + j]])

            def step(k):
                ok = lambda u: 0 <= u < n_u
                if ok(k - 1):
                    S2(k - 1)
                if ok(k + 1):
                    E(k + 1)
                if ok(k):
                    L(k)
                if ok(k - 1):
                    A(k - 1)
                if ok(k - 2):
                    S3(k - 2)
                if ok(k + 2):
                    QK(k + 2)

            return [(lambda k=k: step(k)) for k in range(-2, n_u + 3)]

        def interleave(steps, dense, total_w):
            n = len(steps)
            emitted = 0.0
            alive = dense is not None
            for k, fn in enumerate(steps):
                fn()
                target = total_w * (k + 1) / n
                while alive and emitted < target:
                    try:
                        emitted += next(dense)
                    except StopIteration:
                        alive = False
            if alive:
                for _ in dense:
                    pass

        def chain_gens(*gens):
            for g in gens:
                if g is not None:
                    yield from g

        def mix(g1, W1, g2, W2, scale2=0.3):
            d1 = d2 = 0.0
            a1 = g1 is not None
            a2 = g2 is not None
            while a1 or a2:
                pick1 = a1 and (not a2 or d1 / W1 <= d2 / W2)
                if pick1:
                    try:
                        w = next(g1)
                        d1 += w
                        yield w
                    except StopIteration:
                        a1 = False
                else:
                    try:
                        w = next(g2)
                        d2 += w
                        yield w * scale2
                    except StopIteration:
                        a2 = False

        for n in range(NTOT):
            l, i = divmod(n, NT)
            if i == 0:
                P.epoch = l
                load_params(l)
                for _ in gen_Ah(l, 0):
                    pass
                flush()
            aok = i < 5
            gC = gen_C((n - 1) // NT, (n - 1) % NT, aok) if n > 0 else None
            gAt = gen_At(l, i)
            gAh = gen_Ah(l, i + 1, aok) if i + 1 < NT else None
            tw = (WC_TOTAL if n > 0 else 0) + 0.3 * WAT + (WAH if i + 1 < NT else 0)
            interleave(att_steps(l, i), chain_gens(mix(gC, WC_TOTAL, gAt, WAT), gAh), tw)
            flush()
        for _ in gen_C(DEPTH - 1, NT - 1):
            pass
        flush()
        for b in range(4):
            P.wait("sp", ("dma", s_st[b]["name"], s_st[b]["cnt"]))
        assert wk[0] == len(wseq), (wk[0], len(wseq))
        P.emit(nc, DEPTH)
    return nc


def make_consts():
    bf = ml_dtypes.bfloat16
    j = np.arange(128)[:, None]
    s = np.arange(128)[None, :]
    return {
        "c_ident_bf": np.eye(128).astype(bf),
        "c_ident_f": np.eye(128).astype(np.float32),
        "c_negtri": np.where(j >= s, -1.0, 0.0).astype(bf),
        "c_negones": np.full((128, 128), -1.0).astype(bf),
        "c_cmask": np.where(j < s, 1.0, 0.0).astype(bf),
        "c_avg": np.full((128, 128), 1.0 / 256.0, dtype=np.float32),
    }


_CACHE = {}


def kernel(x, g_attn, w_in, w_conv_a, w_conv_b, b_conv_b, ln_b_g, ln_b_b, w_out, g_ffn, w_ff1, w_ff2, g_final):
    x = np.asarray(x)
    B, T, _ = x.shape
    DEPTH = np.asarray(g_attn).shape[0]
    key = (T, DEPTH)
    if key not in _CACHE:
        _CACHE[key] = build(T, DEPTH)
    nc = _CACHE[key]
    f = lambda a: np.ascontiguousarray(np.asarray(a, dtype=np.float32))
    shared = {
        "g_attn": f(g_attn), "w_in": f(w_in), "w_conv_a": f(w_conv_a), "w_conv_b": f(w_conv_b),
        "b_conv_b": f(b_conv_b), "ln_b_g": f(ln_b_g), "ln_b_b": f(ln_b_b), "w_out": f(w_out),
        "g_ffn": f(g_ffn), "w_ff1": f(w_ff1), "w_ff2": f(w_ff2), "g_final": f(g_final).reshape(1, D),
    }
    shared.update(make_consts())
    in_maps = []
    for b in range(B):
        m = dict(shared)
        m["x"] = f(x[b])
        in_maps.append(m)
    res = run_bass_kernel_spmd(nc, in_maps, core_ids=list(range(B)))
    return np.stack([np.asarray(r["out"]) for r in res.results], axis=0).astype(np.float32)
```

```python
import contextlib
import numpy as np
import ml_dtypes
import concourse.bass as bass
import concourse.mybir as mybir
from concourse.bass_utils import run_bass_kernel_spmd

F32 = mybir.dt.float32
BF16 = mybir.dt.bfloat16
AF = mybir.ActivationFunctionType
ALU = mybir.AluOpType

D = 1024
NCOL = 2816
DFF = 4096
RMS_EPS = 1e-6
LN_EPS = 1e-5
NS = 4
IN_SLOTS = [(0, 4), (4, 8), (8, 12), (12, 16), (16, 18), (18, 22)]


class Buf:
    __slots__ = ("w", "r")

    def __init__(self):
        self.w = None
        self.r = {}


class Prog:
    ENGS = ("pe", "act", "dve", "pool", "sp")

    def __init__(self):
        self.items = {e: [] for e in self.ENGS}
        self.nops = {e: 0 for e in self.ENGS}
        self.signaled = {e: set() for e in self.ENGS}
        self.epoch = 0
        self.op_epoch = {e: [] for e in self.ENGS}
        self.dma_sems = []
        self.waited = {e: {} for e in self.ENGS}

    def op(self, eng, fn, waits=()):
        for w in waits:
            self.wait(eng, w)
        idx = self.nops[eng]
        self.nops[eng] += 1
        self.op_epoch[eng].append(self.epoch)
        self.items[eng].append(("op", fn, idx))
        return (eng, idx)

    def wait(self, eng, tok):
        if tok is None:
            return
        if tok[0] == "dma":
            _, sname, cnt = tok
            key = ("dma", sname)
            if self.waited[eng].get(key, -1) >= cnt:
                return
            self.waited[eng][key] = cnt
            self.items[eng].append(("wdma", sname, cnt))
            return
        src, idx = tok
        key = (src, self.op_epoch[src][idx])
        if self.waited[eng].get(key, -1) >= idx:
            return
        self.waited[eng][key] = idx
        self.signaled[src].add(idx)
        self.items[eng].append(("wait", src, idx))

    def new_dma_sem(self, name):
        self.dma_sems.append(name)
        return {"name": name, "cnt": 0}

    def dma(self, eng, sem, fn, waits=()):
        for w in waits:
            self.wait(eng, w)
        sem["cnt"] += 16
        self.items[eng].append(("dma", fn, sem["name"]))
        return ("dma", sem["name"], sem["cnt"])

    def use(self, eng, fn, reads=(), writes=(), dma_sem=None, nosame=False):
        waits = []
        for b in reads:
            if b.w is not None:
                waits.append(b.w)
        for b in writes:
            if b.w is not None:
                waits.append(b.w)
            for t in b.r.values():
                waits.append(t)
        fw = []
        for w in waits:
            if w[0] != "dma" and w[0] == eng and (eng == "pe" or nosame):
                continue
            fw.append(w)
        if dma_sem is not None:
            tok = self.dma(eng, dma_sem, fn, waits=fw)
        else:
            tok = self.op(eng, fn, waits=fw)
        key = ("dma", tok[1]) if tok[0] == "dma" else tok[0]
        for b in reads:
            b.r[key] = tok
        for b in writes:
            b.w = tok
            b.r = {}
        return tok

    def emit(self, nc, n_epochs):
        counts = {}
        for e in self.ENGS:
            c = {}
            for idx in range(self.nops[e]):
                if idx in self.signaled[e]:
                    ep = self.op_epoch[e][idx]
                    c[ep] = c.get(ep, 0) + 1
                    counts[(e, idx)] = c[ep]
        with contextlib.ExitStack() as st:
            sems = {}
            for e in self.ENGS:
                for ep in range(n_epochs):
                    sems[(e, ep)] = st.enter_context(nc.semaphore(f"s_{e}_{ep}"))
            dsems = {n: st.enter_context(nc.semaphore(f"d_{n}")) for n in self.dma_sems}
            block = st.enter_context(nc.Block())

            def run(engname):
                def body(eng):
                    for it in self.items[engname]:
                        k = it[0]
                        if k == "op":
                            ins = it[1](eng)
                            idx = it[2]
                            if idx in self.signaled[engname]:
                                ins.then_inc(sems[(engname, self.op_epoch[engname][idx])], 1)
                        elif k == "wait":
                            _, src, idx = it
                            eng.wait_ge(sems[(src, self.op_epoch[src][idx])], counts[(src, idx)])
                        elif k == "wdma":
                            eng.wait_ge(dsems[it[1]], it[2])
                        else:
                            it[1](eng).then_inc(dsems[it[2]], 16)
                return body

            block.tensor(run("pe"))
            block.scalar(run("act"))
            block.vector(run("dve"))
            block.gpsimd(run("pool"))
            block.sync(run("sp"))


def build(T, DEPTH):
    NT = T // 512
    NB = T // 128
    nc = bass.Bass("TRN2", target_bir_lowering=False)
    dt = lambda n, s, d, k: nc.dram_tensor(n, s, d, kind=k).ap()
    x_in = dt("x", [T, D], F32, "ExternalInput")
    g_attn = dt("g_attn", [DEPTH, D], F32, "ExternalInput")
    w_in = dt("w_in", [DEPTH, D, NCOL], F32, "ExternalInput")
    w_conv_a = dt("w_conv_a", [DEPTH, 3, 256], F32, "ExternalInput")
    w_conv_b = dt("w_conv_b", [DEPTH, 31, 256], F32, "ExternalInput")
    b_conv_b = dt("b_conv_b", [DEPTH, 256], F32, "ExternalInput")
    ln_b_g = dt("ln_b_g", [DEPTH, 256], F32, "ExternalInput")
    ln_b_b = dt("ln_b_b", [DEPTH, 256], F32, "ExternalInput")
    w_out = dt("w_out", [DEPTH, D, D], F32, "ExternalInput")
    g_ffn = dt("g_ffn", [DEPTH, D], F32, "ExternalInput")
    w_ff1 = dt("w_ff1", [DEPTH, D, DFF], F32, "ExternalInput")
    w_ff2 = dt("w_ff2", [DEPTH, DFF, D], F32, "ExternalInput")
    g_final = dt("g_final", [1, D], F32, "ExternalInput")
    c_ident_bf = dt("c_ident_bf", [128, 128], BF16, "ExternalInput")
    c_ident_f = dt("c_ident_f", [128, 128], F32, "ExternalInput")
    c_negtri = dt("c_negtri", [128, 128], BF16, "ExternalInput")
    c_negones = dt("c_negones", [128, 128], BF16, "ExternalInput")
    c_cmask = dt("c_cmask", [128, 128], BF16, "ExternalInput")
    c_avg = dt("c_avg", [128, 128], F32, "ExternalInput")
    out = dt("out", [T, D], F32, "ExternalOutput")
    xres = dt("xres", [T, D], F32, "Internal")

    P = Prog()
    with contextlib.ExitStack() as st:
        sb = lambda n, s, d: st.enter_context(nc.sbuf_tensor(n, s, d))
        pst = lambda n, s, d: st.enter_context(nc.psum_tensor(n, s, d))

        ring = [sb(f"ring{s}", [128, 4096], BF16) for s in range(NS)]
        ringB = [Buf() for _ in range(NS)]
        kT = sb("kT", [128, 4, T], BF16)
        kTB = [[Buf() for _ in range(NT)] for _ in range(4)]
        Vc = sb("Vc", [128, NB, 512], BF16)
        VcB = [Buf() for _ in range(NB)]
        xt = sb("xt", [128, 4, D], F32)
        xtB = [Buf() for _ in range(4)]
        hn2 = [sb(f"hn{k}", [128, D], BF16) for k in range(2)]
        hn2B = [Buf(), Buf()]
        hT = sb("hT", [128, 8, 512], BF16)
        hTB = Buf()
        qm = [sb(f"qm{k}", [128, 4, 2, 512], BF16) for k in range(2)]
        qmB = [[Buf() for _ in range(4)] for _ in range(2)]
        yT = [sb(f"yT{k}", [128, 8, 512], BF16) for k in range(2)]
        yTB = [[Buf() for _ in range(8)] for _ in range(2)]
        uT = [sb(f"uT{k}", [128, 4, 512], BF16) for k in range(2)]
        uTB = [Buf(), Buf()]
        NSCR = 5
        scr = [sb(f"scr{k}", [128, 512], F32) for k in range(NSCR)]
        scrB = [Buf() for _ in range(NSCR)]
        mbuf = sb("mbuf", [128, 2, 514], F32)
        mbufB = [Buf(), Buf()]
        ubuf = sb("ubuf", [128, 2, 542], F32)
        ubufB = [Buf(), Buf()]
        gcv = [sb(f"gcv{k}", [128, 512], F32) for k in range(2)]
        gcvB = [Buf(), Buf()]
        Eb = [sb(f"Eb{k}", [128, 512], F32) for k in range(2)]
        EbB = [Buf(), Buf()]
        Lb = [sb(f"Lb{k}", [128, 512], BF16) for k in range(2)]
        LbB = [Buf(), Buf()]
        Ab = [sb(f"Ab{k}", [128, 512], BF16) for k in range(2)]
        AbB = [Buf(), Buf()]
        Ls = [sb(f"Ls{k}", [128, 512], BF16) for k in range(2)]
        LsB = [Buf(), Buf()]
        ident_bf = sb("ident_bf", [128, 128], BF16)
        ident_f = sb("ident_f", [128, 128], F32)
        negtri = sb("negtri", [128, 128], BF16)
        negones = sb("negones", [128, 128], BF16)
        cmask = sb("cmask", [128, 128], BF16)
        avgm = sb("avgm", [128, 128], F32)
        constB = Buf()
        gFin = sb("gFin", [128, D], F32)
        gFinB = Buf()
        prm = sb("prm", [37, 256], F32)
        prmB = Buf()
        gst = sb("gst", [8, 2, 128], F32)
        gstB = Buf()
        prmT = sb("prmT", [128, 2, 2, 37], F32)
        prmTB2 = [Buf(), Buf()]
        gT = sb("gT", [128, 2, 2, 8], F32)
        gTB2 = [Buf(), Buf()]
        stat = sb("stat", [128, 16], F32)
        statB = Buf()

        bank = [pst(f"bank{k}", [128, 512], F32) for k in range(8)]
        bankB = [Buf() for _ in range(8)]
        PSA = (0, 1)
        PSB = (2, 3)
        PSO = 4
        DN = (5, 6, 7)

        s_ring = [P.new_dma_sem(f"ring{s}") for s in range(NS)]
        s_x = [P.new_dma_sem(f"x{b}") for b in range(4)]
        s_st = [P.new_dma_sem(f"st{b}") for b in range(4)]
        s_c = P.new_dma_sem("c")
        s_gf = P.new_dma_sem("gf")
        s_gs = P.new_dma_sem("gs")
        s_p = P.new_dma_sem("p")

        xresB = [[Buf() for _ in range(4)] for _ in range(NT)]

        NTOT = DEPTH * NT
        dorder = []
        for n in range(NTOT):
            l_, i_ = divmod(n, NT)
            if i_ == 0:
                dorder.append(("A", l_))
            if n > 0:
                dorder.append(("C", (n - 1) // NT))
            if (n + 1) % NT != 0:
                dorder.append(("A", l_))
        dorder.append(("C", DEPTH - 1))
        wseq = []
        for ph, l in dorder:
            if ph == "A":
                for g in range(6):
                    wseq.append(("in", l, g))
            else:
                for eh in range(2):
                    wseq.append(("out", l, eh))
                order = [("w1", 0), ("w1", 1)]
                for g in range(2, 8):
                    order += [("w2", g - 2), ("w1", g)]
                order += [("w2", 6), ("w2", 7)]
                for kind, g in order:
                    wseq.append((kind, l, g))
        wstate = {"loaded": 0}

        def author_load(k):
            kind, l, g = wseq[k]
            s = k % NS
            slot = ring[s]
            if kind == "in":
                lo, hi = IN_SLOTS[g]
                n = (hi - lo) * 128
                dst = slot[:].rearrange("p (c n) -> p c n", c=8)[:, :, 0:n]
                src = w_in[l].rearrange("(c p) n -> p c n", p=128)[:, :, lo * 128:hi * 128]
            elif kind == "out":
                dst = slot[:].rearrange("p (c n) -> p c n", c=8)
                src = w_out[l].rearrange("(c p) n -> p c n", p=128)[:, :, g * 512:(g + 1) * 512]
            elif kind == "w1":
                dst = slot[:].rearrange("p (c n) -> p c n", c=8)
                src = w_ff1[l].rearrange("(c p) n -> p c n", p=128)[:, :, g * 512:(g + 1) * 512]
            else:
                dst = slot[:].rearrange("p (c n) -> p c n", c=4)
                src = w_ff2[l][g * 512:(g + 1) * 512, :].rearrange("(c p) n -> p c n", p=128)
            P.use("pool", lambda e, dst=dst, src=src: e.dma_start(out=dst, in_=src),
                  writes=[ringB[s]], dma_sem=s_ring[s])

        def get_w(k, kind=None):
            while wstate["loaded"] < min(len(wseq), NS):
                author_load(wstate["loaded"])
                wstate["loaded"] += 1
            assert wstate["loaded"] > k, (k, wstate)
            if kind is not None:
                assert wseq[k][0] == kind[0] and wseq[k][2] == kind[1], (wseq[k], kind)
            return k % NS

        def release(k):
            nk = k + NS
            if nk < len(wseq):
                assert wstate["loaded"] == nk, (k, wstate)
                author_load(nk)
                wstate["loaded"] += 1

        for dst_t, src in ((ident_bf, c_ident_bf), (ident_f, c_ident_f), (negtri, c_negtri),
                           (negones, c_negones), (cmask, c_cmask), (avgm, c_avg)):
            P.use("sp", lambda e, d=dst_t, s_=src: e.dma_start(out=d[:], in_=s_), writes=[constB], dma_sem=s_c)
        P.use("sp", lambda e: e.dma_start(out=gFin[:], in_=g_final.partition_broadcast(128)), writes=[gFinB], dma_sem=s_gf)
        for k in range(2):
            for j in range(4):
                P.use("dve", lambda e, j=j, k=k: e.memset(qm[k][:, j, :, :], 0.0), writes=[qmB[k][j]])

        dctr = {"d": 0, "ev": 0}
        pend = []

        def flush():
            while pend:
                pend.pop(0)()

        def next_bank():
            b = DN[dctr["d"] % 3]
            dctr["d"] += 1
            return b

        def evac_eng():
            dctr["ev"] += 1
            return "act" if dctr["ev"] % 2 else "dve"

        def copy_op(eng, out_ap, in_ap, reads, writes):
            if eng == "act":
                return P.use("act", lambda e: e.activation(out=out_ap, in_=in_ap, func=AF.Copy), reads=reads, writes=writes)
            return P.use("dve", lambda e: e.tensor_copy(out=out_ap, in_=in_ap), reads=reads, writes=writes)

        def rms_stats(b):
            c0 = 3 * b
            P.use("dve", lambda e: e.memset(stat[:, c0:c0 + 1], 0.0), writes=[statB])
            P.use("act", lambda e: e.activation(out=hn2[b % 2][:], in_=xt[:, b, :], func=AF.Square, accum_out=stat[:, c0:c0 + 1]),
                  reads=[xtB[b]], writes=[statB, hn2B[b % 2]])
            P.use("act", lambda e: e.activation(out=stat[:, c0 + 1:c0 + 2], in_=stat[:, c0:c0 + 1], func=AF.Ln,
                                                scale=1.0 / D, bias=RMS_EPS), reads=[statB], writes=[statB])
            P.use("act", lambda e: e.activation(out=stat[:, c0 + 2:c0 + 3], in_=stat[:, c0 + 1:c0 + 2], func=AF.Exp,
                                                scale=-0.5), reads=[statB], writes=[statB])
            return stat[:, c0 + 2:c0 + 3]

        def norm_gen(lp, gk):
            gTB = gTB2[lp]
            rs = []
            for b in range(4):
                rs.append(rms_stats(b))
                yield 4
            gb = gT[:, lp, gk, :].unsqueeze(2).to_broadcast([128, 8, 128])
            for b in range(4):
                hn, hnB = hn2[b % 2], hn2B[b % 2]
                P.use("dve", lambda e, b=b, r=rs[b], hn=hn: e.tensor_scalar_mul(out=hn[:], in0=xt[:, b, :], scalar1=r),
                      reads=[xtB[b], statB], writes=[hnB])
                bk = next_bank()
                pTv = bank[bk][:].bitcast(BF16)
                for c in range(8):
                    P.use("pe", lambda e, c=c, hn=hn, pTv=pTv: e.transpose(out=pTv[:, c * 128:(c + 1) * 128],
                                                                           in_=hn[:, c * 128:(c + 1) * 128], identity=ident_bf[:]),
                          reads=[hnB, constB], writes=[bankB[bk]])
                yield 6
                P.use("dve", lambda e, b=b, pTv=pTv: e.tensor_tensor(out=hT[:, :, b * 128:(b + 1) * 128],
                                                                     in0=pTv.rearrange("p (c n) -> p c n", c=8), in1=gb, op=ALU.mult),
                      reads=[bankB[bk], gTB], writes=[hTB])
                yield 6

        wk = [0]

        def load_x(l, i):
            src = x_in if l == 0 else xres
            for b in range(4):
                rd = [] if l == 0 else [xresB[i][b]]
                r0 = i * 512 + b * 128
                P.use("sp", lambda e, b=b, r0=r0: e.dma_start(out=xt[:, b, :], in_=src[r0:r0 + 128, :]),
                      reads=rd, writes=[xtB[b]], dma_sem=s_x[b])

        def load_params(l):
            lp = l % 2
            prmTB, gTB = prmTB2[lp], gTB2[lp]
            for (r0, r1, src) in ((0, 3, w_conv_a[l]), (3, 34, w_conv_b[l]), (34, 35, b_conv_b[l:l + 1, :]),
                                  (35, 36, ln_b_g[l:l + 1, :]), (36, 37, ln_b_b[l:l + 1, :])):
                P.use("sp", lambda e, r0=r0, r1=r1, src=src: e.dma_start(out=prm[r0:r1, :], in_=src),
                      writes=[prmB], dma_sem=s_p)
            for k, gsrc in enumerate((g_attn, g_ffn)):
                P.use("sp", lambda e, k=k, gsrc=gsrc: e.dma_start(
                    out=gst[:, k, :], in_=gsrc[l:l + 1, :].rearrange("o (c p) -> (o c) p", p=128)),
                    writes=[gstB], dma_sem=s_gs)
            for c in range(2):
                bk = next_bank()
                P.use("pe", lambda e, c=c, bk=bk: e.transpose(out=bank[bk][:, 0:37], in_=prm[0:37, c * 128:(c + 1) * 128],
                                                              identity=ident_f[0:37, 0:37]),
                      reads=[prmB, constB], writes=[bankB[bk]])
                P.use("dve", lambda e, c=c, bk=bk: e.tensor_copy(out=prmT[:, lp, c, :], in_=bank[bk][:, 0:37]),
                      reads=[bankB[bk]], writes=[prmTB])
            for k in range(2):
                bk = next_bank()
                P.use("pe", lambda e, k=k, bk=bk: e.transpose(out=bank[bk][:, 0:8], in_=gst[0:8, k, :],
                                                              identity=ident_f[0:8, 0:8]),
                      reads=[gstB, constB], writes=[bankB[bk]])
                P.use("dve", lambda e, k=k, bk=bk: e.tensor_copy(out=gT[:, lp, k, :], in_=bank[bk][:, 0:8]),
                      reads=[bankB[bk]], writes=[gTB])
            for c in range(2):
                P.use("dve", lambda e, c=c: e.memset(mbuf[:, c, 0:2], 0.0), writes=[mbufB[c]])
                P.use("dve", lambda e, c=c: e.memset(ubuf[:, c, 0:30], 0.0), writes=[ubufB[c]])


        def gen_Ah(l, i, act_ok=True):
            ceng = "act" if act_ok else "dve"
            par = i % 2
            lp = l % 2
            prmTB = prmTB2[lp]
            pv = lambda c, r: prmT[:, lp, c, r:r + 1]
            load_x(l, i)
            for _ in range(4):
                yield 8
            for w in norm_gen(l % 2, 0):
                yield w
            wbase = wk[0]
            wk[0] += 6

            def mm_chunk(cc):
                g = [k for k, (lo, hi) in enumerate(IN_SLOTS) if lo <= cc < hi][0]
                off = cc - IN_SLOTS[g][0]
                s = get_w(wbase + g, ("in", g))
                wv = ring[s][:].rearrange("p (c n) -> p c n", c=8)
                bk = next_bank()
                for c in range(8):
                    P.use("pe", lambda e, c=c: e.matmul(
                        bank[bk][:], lhsT=wv[:, c, off * 128:(off + 1) * 128], rhs=hT[:, c, :],
                        start=(c == 0), stop=(c == 7)),
                        reads=[ringB[s], hTB], writes=[bankB[bk]])
                return bk

            for c in range(2):
                bk = mm_chunk(2 + c)
                flush()
                pend.append(lambda bk=bk: P.use("act", lambda e: e.activation(out=scr[0][:], in_=bank[bk][:], func=AF.Copy),
                                                reads=[bankB[bk]], writes=[scrB[0]]))
                yield 8
                bk = mm_chunk(4 + c)
                flush()

                def ev_ah(bk=bk, c=c):
                    P.use("dve", lambda e: e.tensor_tensor(out=mbuf[:, c, 2:514], in0=bank[bk][:], in1=scr[0][:], op=ALU.mult),
                          reads=[bankB[bk], scrB[0]], writes=[mbufB[c]])
                    P.use("dve", lambda e: e.tensor_scalar_mul(out=scr[1][:], in0=mbuf[:, c, 0:512], scalar1=pv(c, 0)),
                          reads=[mbufB[c], prmTB], writes=[scrB[1]])
                    P.use("dve", lambda e: e.scalar_tensor_tensor(out=scr[2][:], in0=mbuf[:, c, 1:513], scalar=pv(c, 1), in1=scr[1][:],
                                                                  op0=ALU.mult, op1=ALU.add),
                          reads=[mbufB[c], scrB[1]], writes=[scrB[2]])
                    P.use("dve", lambda e: e.scalar_tensor_tensor(out=scr[1][:], in0=mbuf[:, c, 2:514], scalar=pv(c, 2), in1=scr[2][:],
                                                                  op0=ALU.mult, op1=ALU.add),
                          reads=[mbufB[c], scrB[2]], writes=[scrB[1]])
                pend.append(ev_ah)
                yield 8
                bk = mm_chunk(c)
                flush()

                def ev_ab(bk=bk, c=c):
                    P.use("dve", lambda e: e.tensor_tensor(out=yT[par][:, c, :], in0=bank[bk][:], in1=scr[1][:], op=ALU.mult),
                          reads=[bankB[bk], scrB[1]], writes=[yTB[par][c]])
                    P.use("dve", lambda e: e.tensor_copy(out=mbuf[:, c, 0:2], in_=mbuf[:, c, 512:514]),
                          reads=[mbufB[c]], writes=[mbufB[c]])
                pend.append(ev_ab)
                yield 8
            release(wbase + 0)
            for c in range(2):
                bk = mm_chunk(6 + c)
                flush()
                pend.append(lambda bk=bk: P.use("act", lambda e: e.activation(out=scr[0][:], in_=bank[bk][:], func=AF.Copy),
                                                reads=[bankB[bk]], writes=[scrB[0]]))
                yield 8
                bk = mm_chunk(8 + c)
                flush()

                def ev_gate(bk=bk, c=c):
                    P.use("act", lambda e: e.activation(out=scr[1][:], in_=bank[bk][:], func=AF.Exp, scale=-1.0),
                          reads=[bankB[bk]], writes=[scrB[1]])
                    P.use("dve", lambda e: e.tensor_scalar_add(out=scr[2][:], in0=scr[1][:], scalar1=1.0),
                          reads=[scrB[1]], writes=[scrB[2]])
                    P.use("dve", lambda e: e.reciprocal(out=scr[1][:], in_=scr[2][:]), reads=[scrB[2]], writes=[scrB[1]])
                    P.use("dve", lambda e: e.tensor_tensor(out=ubuf[:, c, 30:542], in0=scr[0][:], in1=scr[1][:], op=ALU.mult),
                          reads=[scrB[0], scrB[1]], writes=[ubufB[c]])
                pend.append(ev_gate)
                yield 8
            release(wbase + 1)
            flush()
            for j in range(4):
                bk = mm_chunk(10 + j)
                flush()

                def ev_q(bk=bk, j=j):
                    for hh in range(2):
                        P.use("act", lambda e, hh=hh: e.mul(out=qm[par][hh * 64:(hh + 1) * 64, j, hh, :],
                                                            in_=bank[bk][hh * 64:(hh + 1) * 64, :], mul=0.125),
                              reads=[bankB[bk]], writes=[qmB[par][j]])
                pend.append(ev_q)
                if j == 1:
                    release(wbase + 2)
                yield 8
            for j in range(4):
                bk = mm_chunk(14 + j)
                flush()
                pend.append(lambda bk=bk, j=j: copy_op(ceng, kT[:, j, i * 512:(i + 1) * 512], bank[bk][:],
                                                       reads=[bankB[bk]], writes=[kTB[j][i]]))
                if j == 1:
                    release(wbase + 3)
                if j == 3:
                    release(wbase + 4)
                yield 8
            s5 = get_w(wbase + 5, ("in", 5))
            w5 = ring[s5][:].rearrange("p (c n) -> p c n", c=8)
            for b in range(4):
                bk = next_bank()
                for c in range(8):
                    P.use("pe", lambda e, c=c, bk=bk, b=b: e.matmul(
                        bank[bk][:], lhsT=hT[:, c, b * 128:(b + 1) * 128], rhs=w5[:, c, :],
                        start=(c == 0), stop=(c == 7)), reads=[ringB[s5], hTB], writes=[bankB[bk]])
                flush()
                pend.append(lambda bk=bk, b=b: copy_op(ceng, Vc[:, 4 * i + b, :], bank[bk][:],
                                                       reads=[bankB[bk]], writes=[VcB[4 * i + b]]))
                yield 8
            release(wbase + 5)
            flush()
            yield 1

        def gen_At(l, i):
            par = i % 2
            lp = l % 2
            prmTB = prmTB2[lp]
            pv = lambda c, r: prmT[:, lp, c, r:r + 1]
            chain = []
            for c in range(2):
                chain.append(lambda c=c: P.use("dve", lambda e: e.tensor_scalar(
                    out=gcv[c][:], in0=ubuf[:, c, 0:512], scalar1=pv(c, 3), scalar2=pv(c, 34), op0=ALU.mult, op1=ALU.add),
                    reads=[ubufB[c], prmTB], writes=[gcvB[c]]))
            for k in range(1, 31):
                for c in range(2):
                    chain.append(lambda c=c, k=k: P.use("dve", lambda e: e.scalar_tensor_tensor(
                        out=gcv[c][:], in0=ubuf[:, c, k:k + 512], scalar=pv(c, 3 + k), in1=gcv[c][:], op0=ALU.mult, op1=ALU.add),
                        reads=[ubufB[c], gcvB[c]], writes=[gcvB[c]]))
            for c in range(2):
                chain.append(lambda c=c: P.use("dve", lambda e: e.tensor_copy(out=ubuf[:, c, 0:30], in_=ubuf[:, c, 512:542]),
                                               reads=[ubufB[c]], writes=[ubufB[c]]))

            def do_chain(n):
                for _ in range(n):
                    if chain:
                        chain.pop(0)()

            while chain:
                do_chain(4)
                yield 13
            flush()
            bm = next_bank()
            for c in range(2):
                P.use("pe", lambda e, c=c: e.matmul(bank[bm][:], lhsT=avgm[:], rhs=gcv[c][:], start=(c == 0), stop=(c == 1)),
                      reads=[gcvB[c], constB], writes=[bankB[bm]])
            for c in range(2):
                P.use("dve", lambda e, c=c: e.tensor_tensor(out=gcv[c][:], in0=gcv[c][:], in1=bank[bm][:], op=ALU.subtract),
                      reads=[gcvB[c], bankB[bm]], writes=[gcvB[c]])
                P.use("dve", lambda e, c=c: e.tensor_tensor(out=scr[3 + c][:], in0=gcv[c][:], in1=gcv[c][:], op=ALU.mult),
                      reads=[gcvB[c]], writes=[scrB[3 + c]])
            yield 8
            yield 4
            flush()
            bv = next_bank()
            for c in range(2):
                P.use("pe", lambda e, c=c: e.matmul(bank[bv][:], lhsT=avgm[:], rhs=scr[3 + c][:], start=(c == 0), stop=(c == 1)),
                      reads=[scrB[3 + c], constB], writes=[bankB[bv]])
            P.use("act", lambda e: e.activation(out=scr[1][:], in_=bank[bv][:], func=AF.Ln, bias=LN_EPS),
                  reads=[bankB[bv]], writes=[scrB[1]])
            P.use("act", lambda e: e.activation(out=scr[0][:], in_=scr[1][:], func=AF.Exp, scale=-0.5),
                  reads=[scrB[1]], writes=[scrB[0]])
            yield 8
            for c in range(2):
                P.use("dve", lambda e, c=c: e.tensor_tensor(out=gcv[c][:], in0=gcv[c][:], in1=scr[0][:], op=ALU.mult),
                      reads=[gcvB[c], scrB[0]], writes=[gcvB[c]])
                P.use("dve", lambda e, c=c: e.tensor_scalar(out=gcv[c][:], in0=gcv[c][:], scalar1=pv(c, 35), scalar2=pv(c, 36),
                                                             op0=ALU.mult, op1=ALU.add),
                      reads=[gcvB[c], prmTB], writes=[gcvB[c]])
                P.use("act", lambda e, c=c: e.activation(out=scr[1 + c][:], in_=gcv[c][:], func=AF.Exp, scale=-1.0),
                      reads=[gcvB[c]], writes=[scrB[1 + c]])
                P.use("dve", lambda e, c=c: e.tensor_scalar_add(out=scr[3 + c][:], in0=scr[1 + c][:], scalar1=1.0),
                      reads=[scrB[1 + c]], writes=[scrB[3 + c]])
                P.use("dve", lambda e, c=c: e.reciprocal(out=scr[1 + c][:], in_=scr[3 + c][:]), reads=[scrB[3 + c]], writes=[scrB[1 + c]])
                P.use("dve", lambda e, c=c: e.tensor_tensor(out=yT[par][:, 2 + c, :], in0=gcv[c][:], in1=scr[1 + c][:], op=ALU.mult),
                      reads=[gcvB[c], scrB[1 + c]], writes=[yTB[par][2 + c]])
                yield 2

        WAH = 6 + 56 + 48 + 32 + 96 + 1
        WAT = 17 * 13 + 8 + 4 + 8 + 4

        def gen_C(l, i, act_ok=True):
            par = i % 2
            load_x(l, i)
            yield 3

            def add_x(bk, b, eh):
                P.use("dve", lambda e: e.tensor_tensor(
                    out=xt[:, b, eh * 512:(eh + 1) * 512], in0=bank[bk][:], in1=xt[:, b, eh * 512:(eh + 1) * 512], op=ALU.add),
                    reads=[bankB[bk], xtB[b]], writes=[xtB[b]])

            for eh in range(2):
                s = get_w(wk[0], ("out", eh))
                wv = ring[s][:].rearrange("p (c n) -> p c n", c=8)
                for b in range(4):
                    bk = next_bank()
                    for c in range(8):
                        P.use("pe", lambda e, c=c, bk=bk, b=b, wv=wv: e.matmul(
                            bank[bk][:], lhsT=yT[par][:, c, b * 128:(b + 1) * 128], rhs=wv[:, c, :],
                            start=(c == 0), stop=(c == 7)), reads=[ringB[s], yTB[par][c]], writes=[bankB[bk]])
                    flush()
                    pend.append(lambda bk=bk, b=b, eh=eh: add_x(bk, b, eh))
                    yield 8
                release(wk[0])
                wk[0] += 1
            flush()
            for w in norm_gen(l % 2, 1):
                yield w

            def ffn1(g):
                s = get_w(wk[0], ("w1", g))
                wv = ring[s][:].rearrange("p (c n) -> p c n", c=8)
                ub = g % 2
                for f4 in range(4):
                    bk = next_bank()
                    for c in range(8):
                        P.use("pe", lambda e, c=c, bk=bk, f4=f4: e.matmul(
                            bank[bk][:], lhsT=wv[:, c, f4 * 128:(f4 + 1) * 128], rhs=hT[:, c, :],
                            start=(c == 0), stop=(c == 7)), reads=[ringB[s], hTB], writes=[bankB[bk]])
                    flush()

                    def ev(bk=bk, f4=f4):
                        if act_ok:
                            P.use("act", lambda e: e.activation(out=uT[ub][:, f4, :], in_=bank[bk][:], func=AF.Relu),
                                  reads=[bankB[bk]], writes=[uTB[ub]])
                        else:
                            P.use("dve", lambda e: e.tensor_scalar_max(out=uT[ub][:, f4, :], in0=bank[bk][:], scalar1=0.0),
                                  reads=[bankB[bk]], writes=[uTB[ub]])
                        P.use("dve", lambda e: e.tensor_tensor(out=uT[ub][:, f4, :], in0=bank[bk][:], in1=uT[ub][:, f4, :], op=ALU.mult),
                              reads=[bankB[bk], uTB[ub]], writes=[uTB[ub]])
                    pend.append(ev)
                    yield 8
                release(wk[0])
                wk[0] += 1

            def ffn2(g):
                s = get_w(wk[0], ("w2", g))
                wv2 = ring[s][:].rearrange("p (c n) -> p c n", c=4)
                ub = g % 2
                flush()
                for eh in range(2):
                    for b in range(4):
                        bk = next_bank()
                        for f4 in range(4):
                            P.use("pe", lambda e, f4=f4, bk=bk, b=b, eh=eh: e.matmul(
                                bank[bk][:], lhsT=uT[ub][:, f4, b * 128:(b + 1) * 128], rhs=wv2[:, f4, eh * 512:(eh + 1) * 512],
                                start=(f4 == 0), stop=(f4 == 3)), reads=[ringB[s], uTB[ub]], writes=[bankB[bk]])
                        flush()
                        pend.append(lambda bk=bk, b=b, eh=eh: add_x(bk, b, eh))
                        yield 4
                release(wk[0])
                wk[0] += 1

            yield from ffn1(0)
            yield from ffn1(1)
            for g in range(2, 8):
                yield from ffn2(g - 2)
                yield from ffn1(g)
            yield from ffn2(6)
            yield from ffn2(7)
            flush()
            if l == DEPTH - 1:
                rs = [rms_stats(b) for b in range(4)]
                for b in range(4):
                    P.use("dve", lambda e, b=b, r=rs[b]: e.scalar_tensor_tensor(
                        out=xt[:, b, :], in0=xt[:, b, :], scalar=r, in1=gFin[:], op0=ALU.mult, op1=ALU.mult),
                        reads=[xtB[b], statB, gFinB], writes=[xtB[b]])
                dst = out
            else:
                dst = xres
            for b in range(4):
                wr = [] if l == DEPTH - 1 else [xresB[i][b]]
                r0 = i * 512 + b * 128
                P.use("sp", lambda e, b=b, r0=r0: e.dma_start(out=dst[r0:r0 + 128, :], in_=xt[:, b, :]),
                      reads=[xtB[b]], writes=wr, dma_sem=s_st[b])
            yield 2

        WC_TOTAL = 3 + 64 + 56 + 256 + 256 + 2

        def att_steps(l, i):
            par = i % 2
            units = []
            for j in range(4):
                for hh in range(2):
                    for kb in range(4 * i + 3, -1, -1):
                        units.append((j, hh, kb))
            n_u = len(units)

            def geom(u):
                j, hh, kb = units[u]
                r = kb - 4 * i
                c0 = r * 128 if r >= 0 else 0
                return j, hh, kb, r, c0, (kb == 4 * i + 3), (kb == 0)

            PS = (0, 1, 2, 3)

            def QK(u):
                j, hh, kb, r, c0, first, last = geom(u)
                bk = PS[u % 4]
                P.use("pe", lambda e: e.matmul(bank[bk][:, c0:512], lhsT=kT[:, j, kb * 128:(kb + 1) * 128],
                                               rhs=qm[par][:, j, hh, c0:512], start=True, stop=False, skip_group_check=True),
                      reads=[kTB[j][kb // 4], qmB[par][j]], writes=[bankB[bk]])

            def E(u):
                j, hh, kb, r, c0, first, last = geom(u)
                bk = PS[u % 4]
                P.use("act", lambda e: e.activation(out=Eb[u % 2][:, c0:512], in_=bank[bk][:, c0:512], func=AF.Exp),
                      reads=[bankB[bk]], writes=[EbB[u % 2]])

            def L(u):
                j, hh, kb, r, c0, first, last = geom(u)
                P.use("act", lambda e: e.activation(out=Lb[u % 2][:, c0:512], in_=Eb[u % 2][:, c0:512], func=AF.Ln, bias=1.0),
                      reads=[EbB[u % 2]], writes=[LbB[u % 2]])
                if r >= 0:
                    P.use("dve", lambda e: e.tensor_tensor(out=Lb[u % 2][:, c0:c0 + 128], in0=Lb[u % 2][:, c0:c0 + 128],
                                                           in1=cmask[:], op=ALU.mult),
                          reads=[LbB[u % 2], constB], writes=[LbB[u % 2]])

            def S2(u):
                j, hh, kb, r, c0, first, last = geom(u)
                bk = PS[u % 4]
                lsi = (j * 2 + hh) % 2
                if first:
                    P.use("dve", lambda e: e.memset(Ls[lsi][:], 0.0), writes=[LsB[lsi]])
                P.use("pe", lambda e: e.matmul(bank[bk][:, c0:512], lhsT=negtri[:], rhs=Lb[u % 2][:, c0:512],
                                               start=False, stop=first, skip_group_check=True),
                      reads=[LbB[u % 2], constB, bankB[bk]], writes=[bankB[bk]])
                if not first:
                    P.use("pe", lambda e: e.matmul(bank[bk][:, c0:512], lhsT=negones[:], rhs=Ls[lsi][:, c0:512],
                                                   start=False, stop=True, skip_group_check=True),
                          reads=[LsB[lsi], constB], writes=[bankB[bk]])
                if not last:
                    P.use("dve", lambda e: e.tensor_tensor(out=Ls[lsi][:, c0:512], in0=Ls[lsi][:, c0:512],
                                                           in1=Lb[u % 2][:, c0:512], op=ALU.add),
                          reads=[LsB[lsi], LbB[u % 2]], writes=[LsB[lsi]])

            def A(u):
                j, hh, kb, r, c0, first, last = geom(u)
                bk = PS[u % 4]
                P.use("act", lambda e: e.activation(out=Ab[u % 2][:, c0:512], in_=bank[bk][:, c0:512], func=AF.Exp),
                      reads=[bankB[bk]], writes=[AbB[u % 2]])
                if r >= 0:
                    P.use("dve", lambda e: e.tensor_tensor(out=Ab[u % 2][:, c0:c0 + 128], in0=Ab[u % 2][:, c0:c0 + 128],
                                                           in1=cmask[:], op=ALU.mult),
                          reads=[AbB[u % 2], constB], writes=[AbB[u % 2]])

            def S3(u):
                j, hh, kb, r, c0, first, last = geom(u)
                bk = PSO
                vv = Vc[:, kb, j * 128:(j + 1) * 128]
                P.use("pe", lambda e: e.matmul(bank[bk][:, c0:512], lhsT=vv, rhs=Ab[u % 2][:, c0:512],
                                               start=first, stop=last, skip_group_check=True),
                      reads=[VcB[kb], AbB[u % 2]], writes=[bankB[bk]])
                if last:
                    copy_op("act" if i < 5 else "dve", yT[par][hh * 64:(hh + 1) * 64, 4 + j, :], bank[bk][hh * 64:(hh + 1) * 64, :],
                            reads=[bankB[bk]], writes=[yTB[par][4 + j]])

            def step(k):
                ok = lambda u: 0 <= u < n_u
                if ok(k - 1):
                    S2(k - 1)
                if ok(k + 1):
                    E(k + 1)
                if ok(k):
                    L(k)
                if ok(k - 1):
                    A(k - 1)
                if ok(k - 2):
                    S3(k - 2)
                if ok(k + 2):
                    QK(k + 2)

            return [(lambda k=k: step(k)) for k in range(-2, n_u + 3)]

        def interleave(steps, dense, total_w):
            n = len(steps)
            emitted = 0.0
            alive = dense is not None
            for k, fn in enumerate(steps):
                fn()
                target = total_w * (k + 1) / n
                while alive and emitted < target:
                    try:
                        emitted += next(dense)
                    except StopIteration:
                        alive = False
            if alive:
                for _ in dense:
                    pass

        def chain_gens(*gens):
            for g in gens:
                if g is not None:
                    yield from g

        def mix(g1, W1, g2, W2, scale2=0.3):
            d1 = d2 = 0.0
            a1 = g1 is not None
            a2 = g2 is not None
            while a1 or a2:
                pick1 = a1 and (not a2 or d1 / W1 <= d2 / W2)
                if pick1:
                    try:
                        w = next(g1)
                        d1 += w
                        yield w
                    except StopIteration:
                        a1 = False
                else:
                    try:
                        w = next(g2)
                        d2 += w
                        yield w * scale2
                    except StopIteration:
                        a2 = False

        for n in range(NTOT):
            l, i = divmod(n, NT)
            if i == 0:
                P.epoch = l
                load_params(l)
                for _ in gen_Ah(l, 0):
                    pass
                flush()
            aok = i < 5
            gC = gen_C((n - 1) // NT, (n - 1) % NT, aok) if n > 0 else None
            gAt = gen_At(l, i)
            gAh = gen_Ah(l, i + 1, aok) if i + 1 < NT else None
            tw = (WC_TOTAL if n > 0 else 0) + 0.3 * WAT + (WAH if i + 1 < NT else 0)
            interleave(att_steps(l, i), chain_gens(mix(gC, WC_TOTAL, gAt, WAT), gAh), tw)
            flush()
        for _ in gen_C(DEPTH - 1, NT - 1):
            pass
        flush()
        for b in range(4):
            P.wait("sp", ("dma", s_st[b]["name"], s_st[b]["cnt"]))
        assert wk[0] == len(wseq), (wk[0], len(wseq))
        P.emit(nc, DEPTH)
    return nc


def make_consts():
    bf = ml_dtypes.bfloat16
    j = np.arange(128)[:, None]
    s = np.arange(128)[None, :]
    return {
        "c_ident_bf": np.eye(128).astype(bf),
        "c_ident_f": np.eye(128).astype(np.float32),
        "c_negtri": np.where(j >= s, -1.0, 0.0).astype(bf),
        "c_negones": np.full((128, 128), -1.0).astype(bf),
        "c_cmask": np.where(j < s, 1.0, 0.0).astype(bf),
        "c_avg": np.full((128, 128), 1.0 / 256.0, dtype=np.float32),
    }


_CACHE = {}


def kernel(x, g_attn, w_in, w_conv_a, w_conv_b, b_conv_b, ln_b_g, ln_b_b, w_out, g_ffn, w_ff1, w_ff2, g_final):
    x = np.asarray(x)
    B, T, _ = x.shape
    DEPTH = np.asarray(g_attn).shape[0]
    key = (T, DEPTH)
    if key not in _CACHE:
        _CACHE[key] = build(T, DEPTH)
    nc = _CACHE[key]
    f = lambda a: np.ascontiguousarray(np.asarray(a, dtype=np.float32))
    shared = {
        "g_attn": f(g_attn), "w_in": f(w_in), "w_conv_a": f(w_conv_a), "w_conv_b": f(w_conv_b),
        "b_conv_b": f(b_conv_b), "ln_b_g": f(ln_b_g), "ln_b_b": f(ln_b_b), "w_out": f(w_out),
        "g_ffn": f(g_ffn), "w_ff1": f(w_ff1), "w_ff2": f(w_ff2), "g_final": f(g_final).reshape(1, D),
    }
    shared.update(make_consts())
    in_maps = []
    for b in range(B):
        m = dict(shared)
        m["x"] = f(x[b])
        in_maps.append(m)
    res = run_bass_kernel_spmd(nc, in_maps, core_ids=list(range(B)))
    return np.stack([np.asarray(r["out"]) for r in res.results], axis=0).astype(np.float32)
```
